# Optimizing a Trainium2 kernel written in Bass

```python
import math
import jax, jax.numpy as jnp
from jax import lax
import numpy as np

D_MODEL = 2048
BATCH = 2
SEQ = 4096
DEPTH = 2

HEAD_DIM = 64
N_Q_HEADS = 16
N_KV_HEADS = 4
GQA_GROUP = N_Q_HEADS // N_KV_HEADS
ATTN_WIDTH = N_Q_HEADS * HEAD_DIM
KV_WIDTH = N_KV_HEADS * HEAD_DIM
WINDOW = 128
BLOCK = 128
SSM_WIDTH = D_MODEL // 2
SSM_GROUP_CH = 16
SSM_GROUPS = SSM_WIDTH // SSM_GROUP_CH
SSM_STATE = 64
DT_MIN = 1e-3
DT_MAX = 1e-1
D_FF = -(-8 * D_MODEL // (3 * 256)) * 256
OFF_Q = 0
OFF_K = OFF_Q + ATTN_WIDTH
OFF_V = OFF_K + KV_WIDTH
OFF_U = OFF_V + KV_WIDTH
OFF_G = OFF_U + SSM_WIDTH
IN_WIDTH = OFF_G + 2 * D_MODEL
RMS_EPS = 1e-6

kernel_name = "hybrid_swa_sink_s5_gated_block"


def rmsnorm(x, g):
    xf = x.astype(jnp.float32)
    y = xf * lax.rsqrt(jnp.mean(xf * xf, axis=-1, keepdims=True) + RMS_EPS)
    return (y * g.astype(jnp.float32)).astype(x.dtype)


def alibi_slopes():
    return jnp.exp2(-8.0 * jnp.arange(1, N_Q_HEADS + 1, dtype=jnp.float32) / N_Q_HEADS)


def sliding_window_attention(q, k, v, q_gain, k_gain, sinks):
    B, L = q.shape[0], q.shape[1]
    nb = L // BLOCK
    q = rmsnorm(q, q_gain).astype(jnp.float32)
    k = rmsnorm(k, k_gain).astype(jnp.float32)
    v = v.astype(jnp.float32)
    qb = q.reshape(B, nb, BLOCK, N_KV_HEADS, GQA_GROUP, HEAD_DIM)
    pad = ((0, 0), (BLOCK, 0), (0, 0), (0, 0))
    kp = jnp.pad(k, pad)[:, :L].reshape(B, nb, BLOCK, N_KV_HEADS, HEAD_DIM)
    vp = jnp.pad(v, pad)[:, :L].reshape(B, nb, BLOCK, N_KV_HEADS, HEAD_DIM)
    kb = jnp.concatenate([kp, k.reshape(B, nb, BLOCK, N_KV_HEADS, HEAD_DIM)], axis=2)
    vb = jnp.concatenate([vp, v.reshape(B, nb, BLOCK, N_KV_HEADS, HEAD_DIM)], axis=2)
    scores = jnp.einsum('bnqkgd,bnskd->bnkgqs', qb, kb) * (HEAD_DIM ** -0.5)
    t_loc = jnp.arange(BLOCK)
    s_loc = jnp.arange(2 * BLOCK) - BLOCK
    dist = (t_loc[:, None] - s_loc[None, :]).astype(jnp.float32)
    s_abs = jnp.arange(nb)[:, None] * BLOCK + s_loc[None, :]
    valid = (dist >= 0)[None] & (dist < WINDOW)[None] & (s_abs >= 0)[:, None, :]
    bias = (-alibi_slopes()[:, None, None] * dist[None]).reshape(N_KV_HEADS, GQA_GROUP, BLOCK, 2 * BLOCK)
    scores = jnp.where(valid[None, :, None, None], scores + bias[None, None], -jnp.inf)
    sink = sinks.astype(jnp.float32).reshape(1, 1, N_KV_HEADS, GQA_GROUP, 1, 1)
    m = jnp.maximum(jnp.max(scores, axis=-1, keepdims=True), sink)
    p = jnp.exp(scores - m)
    denom = jnp.sum(p, axis=-1, keepdims=True) + jnp.exp(sink - m)
    out = jnp.einsum('bnkgqs,bnskd->bnqkgd', p / denom, vb)
    return out.reshape(B, L, ATTN_WIDTH)


def s5_ssm(u, lam_re, lam_im, log_dt, b_re, b_im, c_re, c_im, d_skip):
    B, L = u.shape[0], u.shape[1]
    uf = u.astype(jnp.float32).reshape(B, L, SSM_GROUPS, SSM_GROUP_CH)
    lr = lam_re.astype(jnp.float32)
    li = lam_im.astype(jnp.float32)
    dt = jnp.exp(log_dt.astype(jnp.float32))[:, None]
    mag = jnp.exp(lr * dt)
    ar = mag * jnp.cos(li * dt)
    ai = mag * jnp.sin(li * dt)
    den = lr * lr + li * li
    fr = ((ar - 1.0) * lr + ai * li) / den
    fi = (ai * lr - (ar - 1.0) * li) / den
    br = b_re.astype(jnp.float32)
    bi = b_im.astype(jnp.float32)
    bbar_r = fr[:, :, None] * br - fi[:, :, None] * bi
    bbar_i = fr[:, :, None] * bi + fi[:, :, None] * br
    bu_r = jnp.einsum('blgh,gph->blgp', uf, bbar_r)
    bu_i = jnp.einsum('blgh,gph->blgp', uf, bbar_i)
    a_r = jnp.broadcast_to(ar[None, None], (1, L, SSM_GROUPS, SSM_STATE))
    a_i = jnp.broadcast_to(ai[None, None], (1, L, SSM_GROUPS, SSM_STATE))

    def combine(e1, e2):
        ar1, ai1, br1, bi1 = e1
        ar2, ai2, br2, bi2 = e2
        return (ar2 * ar1 - ai2 * ai1,
                ar2 * ai1 + ai2 * ar1,
                ar2 * br1 - ai2 * bi1 + br2,
                ar2 * bi1 + ai2 * br1 + bi2)

    _, _, s_r, s_i = lax.associative_scan(combine, (a_r, a_i, bu_r, bu_i), axis=1)
    y = (jnp.einsum('blgp,ghp->blgh', s_r, c_re.astype(jnp.float32))
         - jnp.einsum('blgp,ghp->blgh', s_i, c_im.astype(jnp.float32))
         + d_skip.astype(jnp.float32).reshape(SSM_GROUPS, SSM_GROUP_CH) * uf)
    return y.reshape(B, L, SSM_WIDTH)


def setup_inputs(seed: int = 0) -> dict:
    key = jax.random.key(seed)
    ks = jax.random.split(key, 24)
    f32 = jnp.float32
    nrm = lambda k, shape, scale: jax.random.normal(k, shape, f32) * scale
    x = jax.random.normal(ks[0], (BATCH, SEQ, D_MODEL), f32)
    norm_mix_g = 1.0 + nrm(ks[1], (DEPTH, D_MODEL), 0.02)
    w_in = nrm(ks[2], (DEPTH, D_MODEL, IN_WIDTH), D_MODEL ** -0.5)
    gate_bias = nrm(ks[3], (DEPTH, 2 * D_MODEL), 0.02)
    q_norm_g = 1.0 + nrm(ks[4], (DEPTH, HEAD_DIM), 0.02)
    k_norm_g = 1.0 + nrm(ks[5], (DEPTH, HEAD_DIM), 0.02)
    attn_sinks = nrm(ks[6], (DEPTH, N_Q_HEADS), 0.5)
    ssm_lambda_re = -0.5 + nrm(ks[7], (DEPTH, SSM_GROUPS, SSM_STATE), 0.01)
    ssm_lambda_im = (math.pi * jnp.arange(SSM_STATE, dtype=f32))[None, None, :] + nrm(ks[8], (DEPTH, SSM_GROUPS, SSM_STATE), 0.01)
    ssm_log_dt = jax.random.uniform(ks[9], (DEPTH, SSM_GROUPS), f32, math.log(DT_MIN), math.log(DT_MAX))
    ssm_b_re = nrm(ks[10], (DEPTH, SSM_GROUPS, SSM_STATE, SSM_GROUP_CH), (2 * SSM_GROUP_CH) ** -0.5)
    ssm_b_im = nrm(ks[11], (DEPTH, SSM_GROUPS, SSM_STATE, SSM_GROUP_CH), (2 * SSM_GROUP_CH) ** -0.5)
    ssm_c_re = nrm(ks[12], (DEPTH, SSM_GROUPS, SSM_GROUP_CH, SSM_STATE), (2 * SSM_STATE) ** -0.5)
    ssm_c_im = nrm(ks[13], (DEPTH, SSM_GROUPS, SSM_GROUP_CH, SSM_STATE), (2 * SSM_STATE) ** -0.5)
    ssm_d = nrm(ks[14], (DEPTH, SSM_WIDTH), 1.0)
    ssm_glu_w = nrm(ks[15], (DEPTH, SSM_WIDTH, SSM_WIDTH), SSM_WIDTH ** -0.5)
    ssm_glu_b = nrm(ks[16], (DEPTH, SSM_WIDTH), 0.02)
    w_attn_branch = nrm(ks[17], (DEPTH, ATTN_WIDTH, D_MODEL), ATTN_WIDTH ** -0.5)
    w_ssm_branch = nrm(ks[18], (DEPTH, SSM_WIDTH, D_MODEL), SSM_WIDTH ** -0.5)
    w_out = nrm(ks[19], (DEPTH, D_MODEL, D_MODEL), D_MODEL ** -0.5)
    norm_ffn_g = 1.0 + nrm(ks[20], (DEPTH, D_MODEL), 0.02)
    w_ffn_in = nrm(ks[21], (DEPTH, D_MODEL, 2 * D_FF), D_MODEL ** -0.5)
    w_ffn_out = nrm(ks[22], (DEPTH, D_FF, D_MODEL), D_FF ** -0.5)
    return {"x": x, "norm_mix_g": norm_mix_g, "w_in": w_in, "gate_bias": gate_bias,
            "q_norm_g": q_norm_g, "k_norm_g": k_norm_g, "attn_sinks": attn_sinks,
            "ssm_lambda_re": ssm_lambda_re, "ssm_lambda_im": ssm_lambda_im, "ssm_log_dt": ssm_log_dt,
            "ssm_b_re": ssm_b_re, "ssm_b_im": ssm_b_im, "ssm_c_re": ssm_c_re, "ssm_c_im": ssm_c_im,
            "ssm_d": ssm_d, "ssm_glu_w": ssm_glu_w, "ssm_glu_b": ssm_glu_b,
            "w_attn_branch": w_attn_branch, "w_ssm_branch": w_ssm_branch, "w_out": w_out,
            "norm_ffn_g": norm_ffn_g, "w_ffn_in": w_ffn_in, "w_ffn_out": w_ffn_out}


def reference(x, norm_mix_g, w_in, gate_bias, q_norm_g, k_norm_g, attn_sinks,
              ssm_lambda_re, ssm_lambda_im, ssm_log_dt, ssm_b_re, ssm_b_im, ssm_c_re, ssm_c_im,
              ssm_d, ssm_glu_w, ssm_glu_b, w_attn_branch, w_ssm_branch, w_out,
              norm_ffn_g, w_ffn_in, w_ffn_out):
    B, L = x.shape[0], x.shape[1]
    for l in range(DEPTH):
        h = rmsnorm(x, norm_mix_g[l])
        z = h @ w_in[l]
        q = z[..., OFF_Q:OFF_K].reshape(B, L, N_Q_HEADS, HEAD_DIM)
        k = z[..., OFF_K:OFF_V].reshape(B, L, N_KV_HEADS, HEAD_DIM)
        v = z[..., OFF_V:OFF_U].reshape(B, L, N_KV_HEADS, HEAD_DIM)
        u = z[..., OFF_U:OFF_G]
        gates = jax.nn.sigmoid(z[..., OFF_G:] + gate_bias[l])
        g_attn = gates[..., :D_MODEL]
        g_ssm = gates[..., D_MODEL:]
        y_attn = sliding_window_attention(q, k, v, q_norm_g[l], k_norm_g[l], attn_sinks[l]).astype(x.dtype)
        y_ssm = s5_ssm(u, ssm_lambda_re[l], ssm_lambda_im[l], ssm_log_dt[l], ssm_b_re[l], ssm_b_im[l],
                       ssm_c_re[l], ssm_c_im[l], ssm_d[l])
        y_ssm = jax.nn.gelu(y_ssm)
        y_ssm = (y_ssm * jax.nn.sigmoid(y_ssm @ ssm_glu_w[l].astype(jnp.float32)
                                        + ssm_glu_b[l].astype(jnp.float32))).astype(x.dtype)
        merged = g_attn * (y_attn @ w_attn_branch[l]) + g_ssm * (y_ssm @ w_ssm_branch[l])
        x = x + merged @ w_out[l]
        h2 = rmsnorm(x, norm_ffn_g[l])
        gu = h2 @ w_ffn_in[l]
        x = x + (jax.nn.silu(gu[..., :D_FF]) * gu[..., D_FF:]) @ w_ffn_out[l]
    return x
```

```python
import math
from contextlib import ExitStack

import numpy as np
import concourse.bass as bass
import concourse.mybir as mybir
from concourse.bass_utils import run_bass_kernel_spmd

F32 = mybir.dt.float32
BF16 = mybir.dt.bfloat16
U8 = mybir.dt.uint8
ALU = mybir.AluOpType
AF = mybir.ActivationFunctionType

NCORES = 8
D = 2048
KT = 16
T = 1024
DEPTH = 2
HD = 64
NQ = 16
NKV = 4
DFF = 5632
FT = DFF // 128
OFF_Q, OFF_K, OFF_V, OFF_U, OFF_G = 0, 1024, 1280, 1536, 2560
INW = 6656
EPS = 1e-6
TWO_PI = 2.0 * math.pi
NEG = -30000.0
PC_GMIX, PC_GFFN, PC_GB, PC_QG, PC_KG, PC_SINK, PC_LR, PC_LI, PC_LDT, PC_D, PC_GLUB = (
    0, 16, 32, 64, 65, 66, 74, 106, 138, 170, 178)
NPC = 186
XS_W = 576
XR_W = 704


class Region:
    __slots__ = ("w", "r")

    def __init__(self):
        self.w = {}
        self.r = {}


class Eng:
    def __init__(self, name, eng, sem, is_pe=False):
        self.name, self.eng, self.sem, self.is_pe = name, eng, sem, is_pe
        self.count = 0
        self.known = {}


class Prog:
    def __init__(self, nc, ctx, n_dma_sems=24):
        self.nc = nc
        mk = lambda n: ctx.enter_context(nc.semaphore(n))
        self.pe = Eng("pe", nc.tensor, mk("s_pe"), True)
        self.act = Eng("act", nc.scalar, mk("s_act"))
        self.dve = Eng("dve", nc.vector, mk("s_dve"))
        self.pool = Eng("pool", nc.gpsimd, mk("s_pool"))
        self.sp = Eng("sp", nc.sync, mk("s_sp"))
        self.engs = [self.pe, self.act, self.dve, self.pool, self.sp]
        self.dsems = [[mk(f"s_dma{i}"), 0] for i in range(n_dma_sems)]
        self.dnext = 0
        self.semobj = {}
        for e in self.engs:
            self.semobj[e.sem.num] = e.sem
        for s, _ in self.dsems:
            self.semobj[s.num] = s

    def _wait(self, E, s, v):
        if E.known.get(s, 0) < v:
            E.eng.wait_ge(self.semobj[s], v)
            E.known[s] = v

    def _waits(self, E, reads, writes):
        need = {}
        for r in reads:
            for s, v in r.w.items():
                need[s] = max(need.get(s, 0), v)
        for r in writes:
            for s, v in r.w.items():
                need[s] = max(need.get(s, 0), v)
            for s, v in r.r.items():
                need[s] = max(need.get(s, 0), v)
        for s, v in need.items():
            if E.is_pe and s == E.sem.num:
                continue
            self._wait(E, s, v)

    def op(self, E, fn, reads=(), writes=(), inc=True):
        self._waits(E, reads, writes)
        ins = fn(E.eng)
        if inc:
            E.count += 1
            ins.then_inc(E.sem, 1)
        c = E.count + (0 if inc else 1)
        for r in reads:
            r.r[E.sem.num] = c
        for r in writes:
            r.w = {E.sem.num: c}
            r.r = {}
        return ins

    def dma(self, Q, out, in_, reads=(), writes=(), **kw):
        self._waits(Q, reads, writes)
        slot = self.dsems[self.dnext]
        self.dnext = (self.dnext + 1) % len(self.dsems)
        sem, tot = slot
        if tot > 0:
            self._wait(Q, sem.num, tot)
        ins = Q.eng.dma_start(out=out, in_=in_, **kw)
        ins.then_inc(sem, 16)
        slot[1] = tot + 16
        for r in reads:
            r.r[sem.num] = slot[1]
        for r in writes:
            r.w = {sem.num: slot[1]}
            r.r = {}
        return ins

    def barrier(self, engs=None):
        engs = engs or self.engs
        for E in engs:
            for F in self.engs:
                if F is not E and F.count > 0:
                    self._wait(E, F.sem.num, F.count)
            for s, tot in self.dsems:
                if tot > 0:
                    self._wait(E, s.num, tot)


def build(n_stage, dbg=None):
    dbg = dbg or {}
    nc = bass.Bass("TRN2", target_bir_lowering=False)
    din = lambda n, s: nc.dram_tensor(n, s, F32, kind="ExternalInput").ap()
    xT = din("xT", [D, T])
    nl_a, nl_b = (n_stage + 1) // 2, n_stage // 2
    w_in = din("w_in", [nl_a, D, INW])
    if nl_b > 0:
        glu_w = din("ssm_glu_w", [nl_b, 1024, 1024])
        w_ab = din("w_attn_branch", [nl_b, 1024, D])
        w_sb = din("w_ssm_branch", [nl_b, 1024, D])
        w_out = din("w_out", [nl_b, D, D])
        w_f1 = din("w_ffn_in", [nl_b, D, 2 * DFF])
        w_f2 = din("w_ffn_out", [nl_b, DFF, D])
    pp_d = din("pp", [128, DEPTH * NPC])
    csm_d = din("csm", [128, DEPTH * 2 * 512])
    bbd_d = din("bbd", [DEPTH * 2, 128, 4096])
    cst_d = din("cst", [128, 128 * 5])
    iota_d = din("iota", [128, 1024])
    dm_d = din("distmask", [128, 512])
    m4_d = din("m4", [128, 8])
    hf_d = din("haloflag", [128, 1])
    xr_d = [din(f"xr{l}", [128, XR_W]) for l in range(nl_b)]
    outs = {}

    def dout(n, s):
        outs[n] = nc.dram_tensor(n, s, F32, kind="ExternalOutput").ap()
        return outs[n]

    n_layers_b = n_stage // 2
    need_xs = (n_stage % 2 == 1)
    if need_xs:
        xs_out = dout("xs", [128, XS_W])
    else:
        outT = dout("outT", [D, T])
    rsp = nc.dram_tensor("rspill", [32, 2, 128, T], F32, kind="Internal").ap()
    for k, shp in dbg.items():
        dout("dbg_" + k, shp)

    with ExitStack() as ctx:
        P = Prog(nc, ctx)
        sbt = lambda n, s, d: ctx.enter_context(nc.sbuf_tensor(n, s, d))
        X = sbt("X", [128, KT, T], F32)
        H = sbt("H", [128, KT, T], BF16)
        U = sbt("U", [128, 8, T], BF16)
        ARENA = sbt("ARENA", [128, 67584], U8)
        WS = sbt("WS", [128, 3, KT, 128], BF16)
        CST = sbt("CST", [128, 5, 128], BF16)
        IOTA = sbt("IOTA", [128, T], F32)
        DM = sbt("DM", [128, 512], F32)
        M4 = sbt("M4", [128, 8], F32)
        HF = sbt("HF", [128, 1], F32)
        PPt = sbt("PP", [128, DEPTH, NPC], F32)
        SM = sbt("SMALL", [128, 32, 32], F32)
        SEED = sbt("SEED", [128, 22, 32], F32)
        rY = Region()
        fin_regs = []
        PSUM = [ctx.enter_context(nc.psum_tensor(f"ps{i}", [128, 1024], F32)) for i in range(4)]
        PSR = [[Region(), Region()] for _ in range(4)]
        psi = [0]

        def ps_next():
            i = psi[0]
            psi[0] = (i + 1) % 4
            return PSUM[i], PSR[i]

        def aview(off, shape, dt):
            n = 1
            for s in shape[1:]:
                n *= s
            nb = n * (4 if dt == F32 else 2)
            v = ARENA[:, off:off + nb]
            v = v.bitcast(dt) if dt != U8 else v
            if len(shape) == 3:
                v = v.rearrange("p (a b) -> p a b", b=shape[2])
            elif len(shape) == 4:
                v = v.rearrange("p (a b c) -> p a b c", b=shape[2], c=shape[3])
            elif len(shape) == 5:
                v = v.rearrange("p (a b c d) -> p a b c d", b=shape[2], c=shape[3], d=shape[4])
            return v

        XS = aview(65280, [128, XS_W], F32)
        WSF = WS[:].rearrange("p a b c -> p (a b c)").bitcast(F32)
        UF = U[:].rearrange("p a b -> p (a b)").bitcast(F32)
        A = lambda fn, r=(), w=(): P.op(P.act, fn, r, w)
        V = lambda fn, r=(), w=(): P.op(P.dve, fn, r, w)
        G = lambda fn, r=(), w=(): P.op(P.pool, fn, r, w)

        def dbg_dump(name, ap, reg):
            if name in dbg:
                P.dma(P.pool, outs["dbg_" + name], ap, reads=reg, writes=[Region()])

        rC = Region()
        P.dma(P.pool, CST[:].rearrange("p a b -> p (a b)"), cst_d[:, :], writes=[rC])
        P.dma(P.sp, IOTA[:], iota_d[:, :], writes=[rC])
        P.dma(P.sp, DM[:], dm_d[:, :], writes=[rC])
        P.dma(P.sp, M4[:], m4_d[:, :], writes=[rC])
        P.dma(P.sp, HF[:], hf_d[:, :], writes=[rC])
        P.dma(P.sp, PPt[:].rearrange("p a b -> p (a b)"), pp_d[:, :], writes=[rC])
        rX = [[Region(), Region()] for _ in range(KT)]
        for kt in range(KT):
            P.dma(P.sp, X[:, kt, :], xT[kt * 128:(kt + 1) * 128, :], writes=rX[kt])
        P.barrier()
        ONES, ONESBD, PERM = CST[:, 0, :], CST[:, 1, :], CST[:, 2, :]
        OPAD = [CST[:, 3, :], CST[:, 4, :]]
        rH = [[Region(), Region()] for _ in range(KT)]
        rU = [[Region(), Region()] for _ in range(8)]
        ws_r = [Region() for _ in range(3)]
        wsi = [0]

        def load_w(Wap, r0, nkt, c0, ncols=128):
            i = wsi[0]
            wsi[0] = (i + 1) % 3
            src = Wap[r0:r0 + nkt * 128, c0:c0 + ncols].rearrange("(kt p) m -> p kt m", p=128)
            P.dma(P.pool, WS[:, i, 0:nkt, 0:ncols], src, writes=[ws_r[i]])
            return WS[:, i], ws_r[i]

        def linear(rhs, rhs_r, nkt, Wap, c0, evac, r0=0, chunks=None):
            chunks = chunks or [(0, nkt)]
            slots = [(load_w(Wap, r0 + a * 128, b - a, c0), a, b) for a, b in chunks]
            ps, pr = ps_next()
            for hf in range(2):
                for (sl, sr), a, b in slots:
                    for kt in range(a, b):
                        P.op(P.pe, lambda e, sl=sl, kt=kt, a=a, hf=hf: e.matmul(
                            ps[:, hf * 512:(hf + 1) * 512], lhsT=sl[:, kt - a, :], rhs=rhs(kt, hf),
                            start=(kt == 0), stop=(kt == nkt - 1)),
                            reads=[sr, rhs_r(kt, hf)], writes=[pr[hf]], inc=(kt == nkt - 1))
            evac(ps, pr)


        def act_rsqrt(out, in_ps, bias_ap, r, w):
            A(lambda e: e.activation(out=out, in_=in_ps, func=AF.Ln, bias=bias_ap), r=r, w=w)
            A(lambda e: e.activation(out=out, in_=out, func=AF.Exp, scale=-0.5), r=w, w=w)

        def rmsnorm(gcol, l):
            sq = aview(0, [128, 2, 512], BF16)
            gs = aview(4096, [128, 16], F32)
            rst = aview(8192, [128, 2, 512], F32)
            rsq = [Region(), Region()]
            rgs = Region()
            rrst = [Region(), Region()]
            V(lambda e: e.tensor_scalar(gs, PPt[:, l, gcol:gcol + 16], math.sqrt(float(D)), None, op0=ALU.mult), w=[rgs])
            for hf in range(2):
                ps, pr = ps_next()
                for kt in range(KT):
                    b = kt % 2
                    A(lambda e, kt=kt, b=b, hf=hf: e.activation(out=sq[:, b, :], in_=X[:, kt, hf * 512:(hf + 1) * 512], func=AF.Square),
                      r=[rX[kt][hf]], w=[rsq[b]])
                    P.op(P.pe, lambda e, kt=kt, b=b: e.matmul(ps[:, 0:512], lhsT=ONES, rhs=sq[:, b, :], start=(kt == 0), stop=(kt == KT - 1)),
                         reads=[rsq[b]], writes=[pr[0]], inc=True)
                act_rsqrt(rst[:, hf, :], ps[:, 0:512], M4[:, 4:5], [pr[0]], [rrst[hf]])
                for kt in range(KT):
                    V(lambda e, kt=kt, hf=hf: e.scalar_tensor_tensor(out=H[:, kt, hf * 512:(hf + 1) * 512], in0=X[:, kt, hf * 512:(hf + 1) * 512],
                                                                      scalar=gs[:, kt:kt + 1], in1=rst[:, hf, :], op0=ALU.mult, op1=ALU.mult),
                      r=[rX[kt][hf], rgs, rrst[hf]], w=[rH[kt][hf]])

        Hrhs = lambda kt, hf: H[:, kt, hf * 512:(hf + 1) * 512]
        Hreg = lambda kt, hf: rH[kt][hf]

        (I_DT, I_LR, I_TH, I_RHO, I_AR, I_AI, I_DEN, I_FR, I_FI, I_T0, I_T1, I_T2, I_T3,
         I_CV, I_SV, I_RER, I_REI, I_SER, I_SEI, I_A1R, I_A1I, I_SIR, I_SII, I_R0R, I_R0I, I_ESK) = range(26)
        smr = Region()

        def sm(i):
            return SM[:, i, :]

        def sdc(k):
            return SEED[:, 2 * k, :]

        def sds(k):
            return SEED[:, 2 * k + 1, :]

        def tt(o, a, b, op):
            V(lambda e: e.tensor_tensor(out=o, in0=a, in1=b, op=op), r=[smr], w=[smr])

        def ts(o, a, s1, s2, op0, op1=None):
            if op1 is None:
                V(lambda e: e.tensor_scalar(o, a, s1, None, op0=op0), r=[smr], w=[smr])
            else:
                V(lambda e: e.tensor_scalar(o, a, s1, s2, op0=op0, op1=op1), r=[smr], w=[smr])

        def cmul(o_r, o_i, a_r, a_i, b_r, b_i):
            tt(sm(I_T2), a_r, b_r, ALU.mult)
            tt(sm(I_T3), a_i, b_i, ALU.mult)
            tt(o_r, sm(I_T2), sm(I_T3), ALU.subtract)
            tt(sm(I_T2), a_r, b_i, ALU.mult)
            tt(sm(I_T3), a_i, b_r, ALU.mult)
            tt(o_i, sm(I_T2), sm(I_T3), ALU.add)

        def csquare(oc, os_, c, s):
            tt(sm(I_T0), c, c, ALU.mult)
            tt(sm(I_T1), s, s, ALU.mult)
            V(lambda e: e.scalar_tensor_tensor(out=sm(I_T2), in0=c, scalar=2.0, in1=s, op0=ALU.mult, op1=ALU.mult), r=[smr], w=[smr])
            tt(oc, sm(I_T0), sm(I_T1), ALU.subtract)
            V(lambda e: e.tensor_copy(out=os_, in_=sm(I_T2)), r=[smr], w=[smr])
            tt(sm(I_T0), oc, oc, ALU.mult)
            tt(sm(I_T1), os_, os_, ALU.mult)
            tt(sm(I_T0), sm(I_T0), sm(I_T1), ALU.add)
            ts(sm(I_T0), sm(I_T0), -0.5, 1.5, ALU.mult, ALU.add)
            tt(oc, oc, sm(I_T0), ALU.mult)
            tt(os_, os_, sm(I_T0), ALU.mult)

        def ssm_small(l):
            ppl = lambda c: PPt[:, l, c:c + 32]
            A(lambda e: e.activation(out=sm(I_DT), in_=ppl(PC_LDT), func=AF.Exp), r=[smr], w=[smr])
            tt(sm(I_LR), ppl(PC_LR), sm(I_DT), ALU.mult)
            tt(sm(I_TH), ppl(PC_LI), sm(I_DT), ALU.mult)
            x = sm(I_LR)
            ts(sm(I_T0), x, 1.0 / 6.0, 1.0, ALU.mult, ALU.add)
            for dv in (5.0, 4.0, 3.0, 2.0):
                tt(sm(I_T0), sm(I_T0), x, ALU.mult)
                ts(sm(I_T0), sm(I_T0), 1.0 / dv, 1.0, ALU.mult, ALU.add)
            tt(sm(I_T0), sm(I_T0), x, ALU.mult)
            ts(sm(I_RHO), sm(I_T0), 1.0, None, ALU.add)
            A(lambda e: e.activation(out=sm(I_T3), in_=sm(I_TH), func=AF.Sin, scale=1.0 / 32.0), r=[smr], w=[smr])
            A(lambda e: e.activation(out=sds(0), in_=sm(I_TH), func=AF.Sin, scale=1.0 / 16.0), r=[smr], w=[smr])
            tt(sm(I_T3), sm(I_T3), sm(I_T3), ALU.mult)
            ts(sdc(0), sm(I_T3), -2.0, 1.0, ALU.mult, ALU.add)
            for _ in range(4):
                csquare(sdc(1), sds(1), sdc(0), sds(0))
                V(lambda e: e.tensor_copy(out=sdc(0), in_=sdc(1)), r=[smr], w=[smr])
                V(lambda e: e.tensor_copy(out=sds(0), in_=sds(1)), r=[smr], w=[smr])
            for k in range(1, 11):
                csquare(sdc(k), sds(k), sdc(k - 1), sds(k - 1))
            tt(sm(I_AR), sm(I_RHO), sdc(0), ALU.mult)
            tt(sm(I_AI), sm(I_RHO), sds(0), ALU.mult)

        def ssm_f(l):
            ppl = lambda c: PPt[:, l, c:c + 32]
            lr, li = ppl(PC_LR), ppl(PC_LI)
            tt(sm(I_T0), lr, lr, ALU.mult)
            tt(sm(I_T1), li, li, ALU.mult)
            tt(sm(I_DEN), sm(I_T0), sm(I_T1), ALU.add)
            V(lambda e: e.reciprocal(out=sm(I_DEN), in_=sm(I_DEN)), r=[smr], w=[smr])
            ts(sm(I_T0), sm(I_AR), -1.0, None, ALU.add)
            tt(sm(I_T1), sm(I_T0), lr, ALU.mult)
            tt(sm(I_T2), sm(I_AI), li, ALU.mult)
            tt(sm(I_T1), sm(I_T1), sm(I_T2), ALU.add)
            tt(sm(I_FR), sm(I_T1), sm(I_DEN), ALU.mult)
            tt(sm(I_T1), sm(I_AI), lr, ALU.mult)
            tt(sm(I_T2), sm(I_T0), li, ALU.mult)
            tt(sm(I_T1), sm(I_T1), sm(I_T2), ALU.subtract)
            tt(sm(I_FI), sm(I_T1), sm(I_DEN), ALU.mult)

        def base_tables(BTc, BTs, rBT):
            t1 = WSF[:, 0:512].rearrange("p (a b) -> p a b", b=16)
            t2 = WSF[:, 512:1024].rearrange("p (a b) -> p a b", b=16)
            V(lambda e: e.memset(BTc[:, :, 0:1], 1.0), w=[rBT])
            V(lambda e: e.memset(BTs[:, :, 0:1], 0.0), w=[rBT])
            for k in range(5):
                n = 1 << k
                cb = sdc(k).unsqueeze(2).to_broadcast([128, 32, n])
                sb_ = sds(k).unsqueeze(2).to_broadcast([128, 32, n])
                src_c, src_s = BTc[:, :, 0:n], BTs[:, :, 0:n]
                V(lambda e: e.tensor_tensor(out=t1[:, :, 0:n], in0=src_s, in1=sb_, op=ALU.mult), r=[rBT, smr], w=[rBT])
                V(lambda e: e.tensor_tensor(out=t2[:, :, 0:n], in0=src_c, in1=cb, op=ALU.mult), r=[rBT, smr], w=[rBT])
                V(lambda e: e.tensor_tensor(out=BTc[:, :, n:2 * n], in0=t2[:, :, 0:n], in1=t1[:, :, 0:n], op=ALU.subtract), r=[rBT], w=[rBT])
                V(lambda e: e.tensor_tensor(out=t1[:, :, 0:n], in0=src_c, in1=sb_, op=ALU.mult), r=[rBT, smr], w=[rBT])
                V(lambda e: e.tensor_tensor(out=t2[:, :, 0:n], in0=src_s, in1=cb, op=ALU.mult), r=[rBT, smr], w=[rBT])
                V(lambda e: e.tensor_tensor(out=BTs[:, :, n:2 * n], in0=t1[:, :, 0:n], in1=t2[:, :, 0:n], op=ALU.add), r=[rBT], w=[rBT])

        def gen_table(gp, cosT, sinT, rT, BTc, BTs, rBT):
            tA, tB = WSF[:, 1024:1536], WSF[:, 1536:2048]
            A(lambda e: e.activation(out=cosT[:, 0:32], in_=BTc[:, gp, :], func=AF.Copy), r=[rBT], w=[rT])
            A(lambda e: e.activation(out=sinT[:, 0:32], in_=BTs[:, gp, :], func=AF.Copy), r=[rBT], w=[rT])
            for k in range(5, 10):
                n = 1 << k
                c_ap, s_ap = SEED[:, 2 * k, gp:gp + 1], SEED[:, 2 * k + 1, gp:gp + 1]
                A(lambda e: e.activation(out=tA[:, 0:n], in_=sinT[:, 0:n], func=AF.Identity, scale=s_ap), r=[rT, smr], w=[rTS[0]])
                A(lambda e: e.activation(out=tB[:, 0:n], in_=cosT[:, 0:n], func=AF.Identity, scale=s_ap), r=[rT, smr], w=[rTS[1]])
                V(lambda e: e.scalar_tensor_tensor(out=cosT[:, n:2 * n], in0=cosT[:, 0:n], scalar=c_ap, in1=tA[:, 0:n], op0=ALU.mult, op1=ALU.subtract),
                  r=[rT, rTS[0], smr], w=[rT])
                V(lambda e: e.scalar_tensor_tensor(out=sinT[:, n:2 * n], in0=sinT[:, 0:n], scalar=c_ap, in1=tB[:, 0:n], op0=ALU.mult, op1=ALU.add),
                  r=[rT, rTS[1], smr], w=[rT])

        rTS = [Region(), Region()]

        def stage_A(l):
            P.barrier()
            rmsnorm(PC_GMIX, l)
            if l == 0:
                dbg_dump("h", H[:, 0, :], [rH[0][0], rH[0][1]])
            for m in range(8):
                def ev(ps, pr, m=m):
                    for hf in range(2):
                        A(lambda e, hf=hf: e.activation(out=U[:, m, hf * 512:(hf + 1) * 512], in_=ps[:, hf * 512:(hf + 1) * 512], func=AF.Copy),
                          r=[pr[hf]], w=[rU[m][hf]])
                linear(Hrhs, Hreg, KT, w_in[l], OFF_U + m * 128, ev)
            P.barrier()
            rXS = Region()
            ksq = aview(0, [128, 128], BF16)
            krin = aview(1024, [128, 128], F32)
            rk = Region()
            for m in range(2):
                sl, sr = load_w(w_in[l], 0, KT, OFF_K + m * 128)
                ps, pr = ps_next()
                for kt in range(KT):
                    P.op(P.pe, lambda e, kt=kt, sl=sl: e.matmul(ps[:, 0:128], lhsT=sl[:, kt, :], rhs=H[:, kt, T - 128:T], start=(kt == 0), stop=(kt == KT - 1)),
                         reads=[sr, rH[kt][1]], writes=[pr[0]], inc=(kt == KT - 1))
                A(lambda e: e.activation(out=ksq, in_=ps[:, 0:128], func=AF.Square), r=[pr[0]], w=[rk])
                P.op(P.pe, lambda e: e.matmul(ps[:, 512:640], lhsT=ONESBD, rhs=ksq, start=True, stop=True), reads=[rk], writes=[pr[1]])
                act_rsqrt(krin, ps[:, 512:640], M4[:, 5:6], [pr[1], rk], [rk])
                V(lambda e, m=m: e.scalar_tensor_tensor(out=XS[:, 64 + m * 128:64 + (m + 1) * 128], in0=ps[:, 0:128], scalar=SM[:, I_ESK, 1:2],
                                                         in1=krin, op0=ALU.mult, op1=ALU.mult), r=[pr[0], rk, smr], w=[rXS])
            ps, pr = ps_next()
            for c in range(2):
                sl, sr = load_w(w_in[l], 0, KT, OFF_V + c * 128)
                for kt in range(KT):
                    P.op(P.pe, lambda e, kt=kt, sl=sl, c=c: e.matmul(ps[:, c * 128:(c + 1) * 128], lhsT=H[:, kt, T - 128:T], rhs=sl[:, kt, :], start=(kt == 0), stop=(kt == KT - 1)),
                         reads=[sr, rH[kt][1]], writes=[pr[0]], inc=(kt == KT - 1))
            A(lambda e: e.activation(out=XS[:, 320:576], in_=ps[:, 0:256], func=AF.Copy), r=[pr[0]], w=[rXS])
            P.barrier()
            BBD = aview(0, [128, 2, 32, 128], BF16)
            BTc = aview(16384, [128, 32, 32], F32)
            BTs = aview(20480, [128, 32, 32], F32)
            TBL = [[aview(24576 + i * 8192, [128, T], F32), aview(24576 + i * 8192 + 4096, [128, T], F32)] for i in range(2)]
            rTBL = [Region(), Region()]
            rB = Region()
            rBT = Region()
            for c in range(2):
                P.dma(P.pool, BBD[:, c].rearrange("p a b -> p (a b)"), bbd_d[l * 2 + c], writes=[rB])
            base_tables(BTc, BTs, rBT)
            TB = 40960
            tmp = [aview(TB + i * 2048, [128, 512], F32) for i in range(10)]
            treg = [Region() for _ in range(10)]
            prev_end = {}
            for gp in range(32):
                kt = gp // 4
                cosT, sinT = TBL[gp % 2]
                rT = rTBL[gp % 2]
                gen_table(gp, cosT, sinT, rT, BTc, BTs, rBT)
                for hf in range(2):
                    o = ((gp * 2 + hf) % 2) * 5
                    bA, bB, bC, rre, rim = [tmp[o + i] for i in range(5)]
                    rA_, rB_, rC_, rrre, rrim = [treg[o + i] for i in range(5)]
                    cs_ = slice(hf * 512, (hf + 1) * 512)
                    cosv, sinv = cosT[:, cs_], sinT[:, cs_]
                    ps, pr = ps_next()
                    for c in range(2):
                        P.op(P.pe, lambda e, c=c: e.matmul(ps[:, c * 512:(c + 1) * 512], lhsT=BBD[:, c, gp, :], rhs=U[:, kt, cs_], start=True, stop=True),
                             reads=[rB, rU[kt][hf]], writes=[pr[c]])
                    V(lambda e: e.tensor_tensor(out=bA, in0=ps[:, 0:512], in1=cosv, op=ALU.mult), r=[pr[0], rT], w=[rA_])
                    V(lambda e: e.tensor_tensor(out=bB, in0=ps[:, 512:1024], in1=sinv, op=ALU.mult), r=[pr[1], rT], w=[rB_])
                    V(lambda e: e.tensor_tensor(out=bA, in0=bA, in1=bB, op=ALU.add), r=[rA_, rB_], w=[rA_])
                    V(lambda e: e.tensor_tensor(out=bC, in0=ps[:, 512:1024], in1=cosv, op=ALU.mult), r=[pr[1], rT], w=[rC_])
                    V(lambda e: e.tensor_tensor(out=bB, in0=ps[:, 0:512], in1=sinv, op=ALU.mult), r=[pr[0], rT, rA_], w=[rB_])
                    V(lambda e: e.tensor_tensor(out=bC, in0=bC, in1=bB, op=ALU.subtract), r=[rC_, rB_], w=[rC_])
                    rho_b = SM[:, I_RHO, gp:gp + 1].to_broadcast([128, 512])
                    if hf == 0:
                        i_re, i_im, ir = 0.0, 0.0, []
                    else:
                        (pre, pim, prr, pri) = prev_end[gp]
                        i_re, i_im, ir = pre[:, 511:512], pim[:, 511:512], [prr, pri]
                    V(lambda e: e.tensor_tensor_scan(out=rre, data0=rho_b, data1=bA, initial=i_re, op0=ALU.mult, op1=ALU.add), r=[rA_, smr] + ir, w=[rrre])
                    V(lambda e: e.tensor_tensor_scan(out=rim, data0=rho_b, data1=bC, initial=i_im, op0=ALU.mult, op1=ALU.add), r=[rC_, smr] + ir, w=[rrim])
                    prev_end[gp] = (rre, rim, rrre, rrim)
                    P.dma(P.sp, rsp[gp, 0, :, cs_], rre, reads=[rrre], writes=[Region()])
                    P.dma(P.sp, rsp[gp, 1, :, cs_], rim, reads=[rrim], writes=[Region()])
                    if hf == 1:
                        A(lambda e: e.activation(out=SM[:, I_CV, gp:gp + 1], in_=cosT[:, T - 1:T], func=AF.Copy), r=[rT, smr], w=[smr])
                        A(lambda e: e.activation(out=SM[:, I_SV, gp:gp + 1], in_=sinT[:, T - 1:T], func=AF.Copy), r=[rT, smr], w=[smr])
                        A(lambda e: e.activation(out=SM[:, I_RER, gp:gp + 1], in_=rre[:, 511:512], func=AF.Copy), r=[rrre, smr], w=[smr])
                        A(lambda e: e.activation(out=SM[:, I_REI, gp:gp + 1], in_=rim[:, 511:512], func=AF.Copy), r=[rrim, smr], w=[smr])
            cmul(sm(I_SER), sm(I_SEI), sm(I_CV), sm(I_SV), sm(I_RER), sm(I_REI))
            A(lambda e: e.activation(out=XS[:, 0:32], in_=sm(I_SER), func=AF.Copy), r=[smr], w=[rXS])
            A(lambda e: e.activation(out=XS[:, 32:64], in_=sm(I_SEI), func=AF.Copy), r=[smr], w=[rXS])
            if need_xs and l == n_stage // 2:
                r_ = Region()
                P.dma(P.sp, xs_out[:, :], XS, reads=[rXS], writes=[r_])
                fin_regs.append(r_)
            P.barrier()

        def stage_B(l):
            ppl = lambda c, n: PPt[:, l, c:c + n]
            P.barrier()
            rXR = Region()
            SPV = SM[:, 26:32, :].rearrange("p a b -> p (a b)")
            P.dma(P.sp, SPV, xr_d[l][:, 0:192], writes=[rXR])
            P.barrier()
            A(lambda e: e.activation(out=sm(I_DEN), in_=sm(I_LR), func=AF.Exp, scale=1024.0), r=[smr], w=[smr])
            tt(sm(I_A1R), sm(I_DEN), sdc(10), ALU.mult)
            tt(sm(I_A1I), sm(I_DEN), sds(10), ALU.mult)
            V(lambda e: e.tensor_copy(out=sm(I_SIR), in_=SPV[:, 128:160]), r=[rXR, smr], w=[smr])
            V(lambda e: e.tensor_copy(out=sm(I_SII), in_=SPV[:, 160:192]), r=[rXR, smr], w=[smr])
            for k in (1, 0):
                cmul(sm(I_R0R), sm(I_R0I), sm(I_SIR), sm(I_SII), sm(I_A1R), sm(I_A1I))
                tt(sm(I_SIR), sm(I_R0R), SPV[:, k * 64:k * 64 + 32], ALU.add)
                tt(sm(I_SII), sm(I_R0I), SPV[:, k * 64 + 32:k * 64 + 64], ALU.add)
            cmul(sm(I_R0R), sm(I_R0I), sm(I_SIR), sm(I_SII), sdc(0), sds(0))
            ssm_f(l)
            WC = aview(0, [128, 2, 32, 128], BF16)
            SB = aview(16384, [128, 4, 2, T], BF16)
            BTc = aview(32768, [128, 32, 32], F32)
            BTs = aview(36864, [128, 32, 32], F32)
            CS_ = aview(40960, [128, 2, 32, 16], F32)
            CF = aview(45056, [128, 2, 32, 16], F32)
            CT = aview(49152, [128, 2, 32, 16], F32)
            rc = Region()
            P.dma(P.sp, CS_[:].rearrange("p a b c -> p (a b c)"), csm_d[:, l * 1024:(l + 1) * 1024], writes=[rc])
            frb = SM[:, I_FR, :].unsqueeze(2).to_broadcast([128, 32, 16])
            fib = SM[:, I_FI, :].unsqueeze(2).to_broadcast([128, 32, 16])
            V(lambda e: e.tensor_tensor(out=CT[:, 0], in0=CS_[:, 0], in1=frb, op=ALU.mult), r=[rc, smr], w=[rc])
            V(lambda e: e.tensor_tensor(out=CT[:, 1], in0=CS_[:, 1], in1=fib, op=ALU.mult), r=[rc, smr], w=[rc])
            V(lambda e: e.tensor_tensor(out=CF[:, 0], in0=CT[:, 0], in1=CT[:, 1], op=ALU.subtract), r=[rc], w=[rc])
            V(lambda e: e.tensor_tensor(out=CT[:, 0], in0=CS_[:, 0], in1=fib, op=ALU.mult), r=[rc, smr], w=[rc])
            V(lambda e: e.tensor_tensor(out=CT[:, 1], in0=CS_[:, 1], in1=frb, op=ALU.mult), r=[rc, smr], w=[rc])
            V(lambda e: e.tensor_tensor(out=CF[:, 1], in0=CT[:, 0], in1=CT[:, 1], op=ALU.add), r=[rc], w=[rc])
            G(lambda e: e.memset(WC[:].rearrange("p a b c -> p (a b c)"), 0.0), w=[rc])
            WC6 = WC.rearrange("p c (k q) (qc g h) -> p c k q qc g h", q=4, qc=4, g=2)
            CF4 = CF.rearrange("p c (k q) h -> p c k q h", q=4)
            for c in range(2):
                for q in range(4):
                    for g2 in range(2):
                        mcol = g2 + (2 if c == 1 else 0)
                        V(lambda e, c=c, q=q, g2=g2, mcol=mcol: e.tensor_scalar(WC6[:, c, :, q, q, g2, :], CF4[:, c, :, q, :], M4[:, mcol:mcol + 1], None, op0=ALU.mult),
                          r=[rc], w=[rc])
            rBT = Region()
            base_tables(BTc, BTs, rBT)
            P.barrier()
            TBL = [[aview(40960 + i * 8192, [128, T], F32), aview(40960 + i * 8192 + 4096, [128, T], F32)] for i in range(2)]
            rTBL = [Region(), Region()]
            tmp = [aview(57344 + i * 2048, [128, 512], F32) for i in range(5)]
            treg = [Region() for _ in range(5)]
            rSB = [Region() for _ in range(4)]
            for kt in range(8):
                for q in range(4):
                    gp = kt * 4 + q
                    cosT, sinT = TBL[gp % 2]
                    rT = rTBL[gp % 2]
                    gen_table(gp, cosT, sinT, rT, BTc, BTs, rBT)
                    for hf in range(2):
                        cs_ = slice(hf * 512, (hf + 1) * 512)
                        cosv, sinv = cosT[:, cs_], sinT[:, cs_]
                        rl_re, rl_im, dec, t1, t2 = tmp
                        r_re, r_im, rdec, rt1, rt2 = treg
                        P.dma(P.sp, rl_re, rsp[gp, 0, :, cs_], writes=[r_re])
                        P.dma(P.sp, rl_im, rsp[gp, 1, :, cs_], writes=[r_im])
                        A(lambda e: e.activation(out=dec, in_=IOTA[:, cs_], func=AF.Exp, scale=SM[:, I_LR, gp:gp + 1], bias=SM[:, I_LR, gp:gp + 1]),
                          r=[smr], w=[rdec])
                        V(lambda e: e.scalar_tensor_tensor(out=rl_re, in0=dec, scalar=SM[:, I_R0R, gp:gp + 1], in1=rl_re, op0=ALU.mult, op1=ALU.add), r=[rdec, smr, r_re], w=[r_re])
                        V(lambda e: e.scalar_tensor_tensor(out=rl_im, in0=dec, scalar=SM[:, I_R0I, gp:gp + 1], in1=rl_im, op0=ALU.mult, op1=ALU.add), r=[rdec, smr, r_im], w=[r_im])
                        G(lambda e: e.tensor_tensor(out=t1, in0=cosv, in1=rl_re, op=ALU.mult), r=[rT, r_re], w=[rt1])
                        G(lambda e: e.tensor_tensor(out=t2, in0=sinv, in1=rl_im, op=ALU.mult), r=[rT, r_im], w=[rt2])
                        V(lambda e: e.tensor_tensor(out=SB[:, q, 0, cs_], in0=t1, in1=t2, op=ALU.subtract), r=[rt1, rt2], w=[rSB[q]])
                        G(lambda e: e.tensor_tensor(out=t1, in0=cosv, in1=rl_im, op=ALU.mult), r=[rT, r_im], w=[rt1])
                        G(lambda e: e.tensor_tensor(out=t2, in0=sinv, in1=rl_re, op=ALU.mult), r=[rT, r_re], w=[rt2])
                        V(lambda e: e.tensor_tensor(out=SB[:, q, 1, cs_], in0=t1, in1=t2, op=ALU.add), r=[rt1, rt2], w=[rSB[q]])
                for hf in range(2):
                    cs_ = slice(hf * 512, (hf + 1) * 512)
                    ps, pr = ps_next()
                    n = 0
                    for q in range(4):
                        for c in range(2):
                            P.op(P.pe, lambda e, q=q, c=c, n=n: e.matmul(ps[:, 0:512], lhsT=WC[:, c, kt * 4 + q, :], rhs=SB[:, q, c, cs_], start=(n == 0), stop=(n == 7)),
                                 reads=[rc, rSB[q]], writes=[pr[0]], inc=(n == 7))
                            n += 1
                    yv, y2, yt = WSF[:, 0:512], WSF[:, 512:1024], WSF[:, 2048:2560]
                    ry = rY
                    V(lambda e: e.scalar_tensor_tensor(out=yv, in0=U[:, kt, cs_], scalar=ppl(PC_D + kt, 1), in1=ps[:, 0:512], op0=ALU.mult, op1=ALU.add),
                      r=[rU[kt][hf], pr[0]], w=[ry])
                    A(lambda e: e.activation(out=y2, in_=yv, func=AF.Square), r=[ry], w=[ry])
                    V(lambda e: e.tensor_scalar(y2, y2, 0.044715, 1.0, op0=ALU.mult, op1=ALU.add), r=[ry], w=[ry])
                    V(lambda e: e.tensor_tensor(out=yt, in0=y2, in1=yv, op=ALU.mult), r=[ry], w=[ry])
                    A(lambda e: e.activation(out=y2, in_=yt, func=AF.Sigmoid, scale=2.0 * math.sqrt(2.0 / math.pi)), r=[ry], w=[ry])
                    V(lambda e: e.tensor_tensor(out=U[:, kt, cs_], in0=yv, in1=y2, op=ALU.mult), r=[ry], w=[rU[kt][hf]])
            if l == 0:
                dbg_dump("yg", U[:, 0, :], [rU[0][0], rU[0][1]])
            P.barrier()

            Q = aview(0, [128, 8, T], BF16)
            KD = aview(16384, [128, 2, 2, 1152], BF16)
            VP = aview(32768, [128, 9, 4, 2, 128], BF16)
            YA = aview(51200, [128, 8, T], BF16)
            qsq = aview(51200, [128, T], BF16)
            qrin = aview(51200 + 2048, [128, T], F32)
            rQ = [Region() for _ in range(8)]
            rKD = Region()
            rVP = Region()
            rq = Region()
            XRH = aview(25600, [128, 512], F32)
            P.dma(P.sp, XRH, xr_d[l][:, 192:704], writes=[rXR])
            G(lambda e: e.memset(VP[:].rearrange("p a b c d -> p (a b c d)"), 0.0), w=[rVP])

            def qk_evac(ps, pr, dst, gcol, ncol=T):
                A(lambda e: e.activation(out=qsq[:, 0:ncol], in_=ps[:, 0:ncol], func=AF.Square), r=[pr[0], pr[1]], w=[rq])
                ps2, pr2 = ps_next()
                for hf in range((ncol + 511) // 512):
                    w_ = min(512, ncol - hf * 512)
                    P.op(P.pe, lambda e, hf=hf, w_=w_: e.matmul(ps2[:, hf * 512:hf * 512 + w_], lhsT=ONESBD, rhs=qsq[:, hf * 512:hf * 512 + w_], start=True, stop=True),
                         reads=[rq], writes=[pr2[hf]])
                act_rsqrt(qrin[:, 0:ncol], ps2[:, 0:ncol], M4[:, 5:6], [pr2[0], pr2[1], rq], [rq])
                V(lambda e: e.scalar_tensor_tensor(out=dst, in0=ps[:, 0:ncol], scalar=gcol, in1=qrin[:, 0:ncol], op0=ALU.mult, op1=ALU.mult),
                  r=[pr[0], pr[1], rq, smr], w=[rq])

            for m in range(8):
                def ev(ps, pr, m=m):
                    qk_evac(ps, pr, Q[:, m, :], ppl(PC_QG, 1))
                    rQ[m].w = dict(rq.w)
                linear(Hrhs, Hreg, KT, w_in[l], OFF_Q + m * 128, ev)
            for m in range(2):
                def ev(ps, pr, m=m):
                    qk_evac(ps, pr, KD[:, 0, m, 128:1152], SM[:, I_ESK, 1:2])
                    rKD.w = dict(rq.w)
                linear(Hrhs, Hreg, KT, w_in[l], OFF_K + m * 128, ev)
            for m in range(2):
                V(lambda e, m=m: e.tensor_copy(out=KD[:, 0, m, 0:128], in_=XRH[:, m * 128:(m + 1) * 128]), r=[rXR, rKD], w=[rKD])
            for m in range(2):
                for c0 in range(0, 1152, 384):
                    ps, pr = ps_next()
                    P.op(P.pe, lambda e, m=m, c0=c0: e.matmul(ps[:, 0:384], lhsT=PERM, rhs=KD[:, 0, m, c0:c0 + 384], start=True, stop=True), reads=[rKD], writes=[pr[0]])
                    A(lambda e, m=m, c0=c0: e.activation(out=KD[:, 1, m, c0:c0 + 384], in_=ps[:, 0:384], func=AF.Copy), r=[pr[0]], w=[rKD])
            wv = [load_w(w_in[l], 0, KT, OFF_V + c * 128) for c in range(2)]
            for blk in range(1, 9):
                ps, pr = ps_next()
                tk = blk - 1
                for c in range(2):
                    sl, sr = wv[c]
                    for kt in range(KT):
                        P.op(P.pe, lambda e, kt=kt, sl=sl, c=c, tk=tk: e.matmul(ps[:, c * 128:(c + 1) * 128], lhsT=H[:, kt, tk * 128:(tk + 1) * 128], rhs=sl[:, kt, :], start=(kt == 0), stop=(kt == KT - 1)),
                             reads=[sr, rH[kt][tk // 4]], writes=[pr[0]], inc=(kt == KT - 1))
                A(lambda e, blk=blk: e.activation(out=VP[:, blk, :, 0, 0:64], in_=ps[:, 0:256].rearrange("p (k d) -> p k d", d=64), func=AF.Copy), r=[pr[0]], w=[rVP])
                V(lambda e, blk=blk: e.tensor_copy(out=VP[:, blk, :, 1, 64:128], in_=ps[:, 0:256].rearrange("p (k d) -> p k d", d=64)), r=[pr[0]], w=[rVP])
            A(lambda e: e.activation(out=VP[:, 0, :, 0, 0:64], in_=XRH[:, 256:512].rearrange("p (k d) -> p k d", d=64), func=AF.Copy), r=[rXR], w=[rVP])
            V(lambda e: e.tensor_copy(out=VP[:, 0, :, 1, 64:128], in_=XRH[:, 256:512].rearrange("p (k d) -> p k d", d=64)), r=[rXR], w=[rVP])
            P.barrier()
            ESK = WSF[:, 1024:1032]
            rE = Region()
            A(lambda e: e.activation(out=ESK, in_=ppl(PC_SINK, 8), func=AF.Exp), w=[rE])
            BHs = [aview(27648, [128, 256], F32), aview(28672, [128, 256], F32)]
            LGs = [aview(29696, [128, 256], F32), aview(30720, [128, 256], F32)]
            PXs = [aview(31744, [128, 256], BF16), aview(32256, [128, 256], BF16)]
            rBH = [Region(), Region()]
            rLG = [Region(), Region()]
            rPX = [Region(), Region()]
            rYA = [Region() for _ in range(8)]
            acc, racc = PSUM[0], PSR[0]
            den, rden = PSUM[1], PSR[1]
            sps, rsps = PSUM[2], PSR[2]
            s4 = [Region() for _ in range(4)]
            rDT = [Region(), Region()]
            sidx = 0
            for hp in range(8):
                kv = hp // 2
                for hf in range(2):
                    V(lambda e, hf=hf: e.memset(acc[:, hf * 512:(hf + 1) * 512], 0.0), w=[racc[hf]])
                    V(lambda e, hf=hf: e.memset(den[:, hf * 512:(hf + 1) * 512], 0.0), w=[rden[hf]])
                for e_ in range(2):
                    h = 2 * hp + e_
                    b0 = 64 * e_
                    slope = 2.0 ** (-8.0 * (h + 1) / NQ)
                    bi = h % 2
                    V(lambda e, bi=bi, slope=slope: e.scalar_tensor_tensor(out=BHs[bi], in0=DM[:, 0:256], scalar=-slope, in1=DM[:, 256:512], op0=ALU.mult, op1=ALU.add),
                      w=[rBH[bi]])
                    ksel = 0 if (kv % 2) == e_ else 1
                    for j in range(9):
                        qlo, qhi = max(0, 128 * (j - 1)), min(T, 128 * (j + 1))
                        ncol = qhi - qlo
                        bc0 = 128 if j == 0 else 0
                        si = sidx % 4
                        sidx += 1
                        S = sps[:, si * 256:si * 256 + ncol]
                        P.op(P.pe, lambda e, S=S, j=j, qlo=qlo, qhi=qhi: e.matmul(S, lhsT=KD[b0:b0 + 64, ksel, kv // 2, j * 128:(j + 1) * 128], rhs=Q[b0:b0 + 64, hp, qlo:qhi], start=True, stop=True),
                             reads=[rKD, rQ[hp]], writes=[s4[si]])
                        li = sidx % 2
                        if j == 0:
                            V(lambda e, S=S, li=li: e.scalar_tensor_tensor(out=LGs[li][:, 0:ncol], in0=S, scalar=HF[:, 0:1], in1=BHs[bi][:, bc0:bc0 + ncol], op0=ALU.add, op1=ALU.add),
                              r=[s4[si], rBH[bi]], w=[rLG[li]])
                        else:
                            V(lambda e, S=S, li=li: e.tensor_tensor(out=LGs[li][:, 0:ncol], in0=S, in1=BHs[bi][:, bc0:bc0 + ncol], op=ALU.add),
                              r=[s4[si], rBH[bi]], w=[rLG[li]])
                        A(lambda e, li=li: e.activation(out=PXs[li][:, 0:ncol], in_=LGs[li][:, 0:ncol], func=AF.Exp), r=[rLG[li]], w=[rPX[li]])
                        for part in range(ncol // 128):
                            n = (qlo // 128) + part
                            is_diag = (n == j - 1)
                            first = (e_ == 0 and not is_diag) or (e_ == 0 and n == 0 and j == 1 and False)
                            first = (e_ == 0 and j == n)
                            last = (e_ == 1 and j == n + 1)
                            pcols = PXs[li][:, part * 128:(part + 1) * 128]
                            P.op(P.pe, lambda e, n=n, pcols=pcols, first=first, last=last, j=j: e.matmul(acc[:, n * 128:(n + 1) * 128], lhsT=VP[:, j, kv, e_, :], rhs=pcols, start=False, stop=last, skip_group_check=True),
                                 reads=[rVP, rPX[li]], writes=[racc[n // 4]])
                            P.op(P.pe, lambda e, n=n, pcols=pcols, first=first, last=last: e.matmul(den[:, n * 128:(n + 1) * 128], lhsT=OPAD[e_], rhs=pcols, start=False, stop=last, skip_group_check=True),
                                 reads=[rPX[li]], writes=[rden[n // 4]])
                for hf in range(2):
                    dtm = WSF[:, hf * 512:(hf + 1) * 512]
                    rdt = rDT[hf]
                    A(lambda e, hf=hf: e.activation(out=dtm, in_=den[:, hf * 512:(hf + 1) * 512], func=AF.Ln, bias=ESK[:, hp:hp + 1]), r=[rden[hf], rE], w=[rdt])
                    A(lambda e, hf=hf: e.activation(out=dtm, in_=dtm, func=AF.Exp, scale=-1.0), r=[rdt], w=[rdt])
                    V(lambda e, hf=hf: e.tensor_tensor(out=YA[:, hp, hf * 512:(hf + 1) * 512], in0=acc[:, hf * 512:(hf + 1) * 512], in1=dtm, op=ALU.mult), r=[racc[hf], rdt], w=[rYA[hp]])
            if l == 0:
                dbg_dump("ya", YA[:, 0, :], [rYA[0]])
            P.barrier()
            YS = aview(32768, [128, 8, T], BF16)
            gate = aview(0, [128, T], F32)
            rYS = [Region() for _ in range(8)]
            rg = Region()
            Urhs = lambda kt, hf: U[:, kt, hf * 512:(hf + 1) * 512]
            Ureg = lambda kt, hf: rU[kt][hf]
            for m in range(8):
                def ev(ps, pr, m=m):
                    A(lambda e: e.activation(out=gate, in_=ps[:, :], func=AF.Sigmoid, bias=ppl(PC_GLUB + m, 1)), r=[pr[0], pr[1]], w=[rg])
                    V(lambda e: e.tensor_tensor(out=YS[:, m, :], in0=U[:, m, :], in1=gate, op=ALU.mult), r=[rg, rU[m][0], rU[m][1]], w=[rYS[m]])
                linear(Urhs, Ureg, 8, glu_w[l], m * 128, ev)
            P.barrier()
            MG = aview(0, [128, KT, T], BF16)
            rMG = [Region() for _ in range(KT)]
            rt = [Region(), Region(), Region()]
            YArhs = lambda kt, hf: YA[:, kt, hf * 512:(hf + 1) * 512]
            YAreg = lambda kt, hf: rYA[kt]
            YSrhs = lambda kt, hf: YS[:, kt, hf * 512:(hf + 1) * 512]
            YSreg = lambda kt, hf: rYS[kt]
            for m in range(KT):
                st = {}

                def ev_ga(ps, pr, m=m, st=st):
                    st["ga"] = (ps, pr)

                def ev_a(ps, pr, m=m, st=st):
                    st["a"] = (ps, pr)

                def ev_gs(ps, pr, m=m, st=st):
                    st["gs"] = (ps, pr)

                def ev_s(ps, pr, m=m, st=st):
                    st["s"] = (ps, pr)
                linear(Hrhs, Hreg, KT, w_in[l], OFF_G + m * 128, ev_ga)
                linear(YArhs, YAreg, 8, w_ab[l], m * 128, ev_a)
                linear(Hrhs, Hreg, KT, w_in[l], OFF_G + D + m * 128, ev_gs)
                linear(YSrhs, YSreg, 8, w_sb[l], m * 128, ev_s)
                for hf in range(2):
                    cs_ = slice(hf * 512, (hf + 1) * 512)
                    g1, t1, g2 = UF[:, 0:512], UF[:, 512:1024], UF[:, 1024:1536]
                    A(lambda e: e.activation(out=g1, in_=st["ga"][0][:, cs_], func=AF.Sigmoid, bias=ppl(PC_GB + m, 1)), r=[st["ga"][1][hf]], w=[rt[0]])
                    V(lambda e: e.tensor_tensor(out=t1, in0=st["a"][0][:, cs_], in1=g1, op=ALU.mult), r=[st["a"][1][hf], rt[0]], w=[rt[1]])
                    A(lambda e: e.activation(out=g2, in_=st["gs"][0][:, cs_], func=AF.Sigmoid, bias=ppl(PC_GB + 16 + m, 1)), r=[st["gs"][1][hf]], w=[rt[2]])
                    V(lambda e: e.tensor_tensor(out=g2, in0=st["s"][0][:, cs_], in1=g2, op=ALU.mult), r=[st["s"][1][hf], rt[2]], w=[rt[2]])
                    V(lambda e: e.tensor_tensor(out=MG[:, m, cs_], in0=t1, in1=g2, op=ALU.add), r=[rt[1], rt[2]], w=[rMG[m]])
            P.barrier()
            MGrhs = lambda kt, hf: MG[:, kt, hf * 512:(hf + 1) * 512]
            MGreg = lambda kt, hf: rMG[kt]
            for m in range(KT):
                def ev(ps, pr, m=m):
                    for hf in range(2):
                        cs_ = slice(hf * 512, (hf + 1) * 512)
                        V(lambda e: e.tensor_tensor(out=X[:, m, cs_], in0=ps[:, cs_], in1=X[:, m, cs_], op=ALU.add), r=[pr[hf], rX[m][hf]], w=[rX[m][hf]])
                linear(MGrhs, MGreg, KT, w_out[l], m * 128, ev)
            if l == 0:
                dbg_dump("xmid", X[:, 0, :], [rX[0][0], rX[0][1]])
            P.barrier()
            rmsnorm(PC_GFFN, l)
            P.barrier()
            ACTB = aview(16384, [128, 22, T], BF16)
            sg = aview(0, [128, 2, 512], F32)
            rACT = [Region() for _ in range(22)]
            for fh in range(2):
                for f in range(22):
                    fg = fh * 22 + f
                    st = {}
                    linear(Hrhs, Hreg, KT, w_f1[l], fg * 128, lambda ps, pr, st=st: st.__setitem__("g", (ps, pr)))
                    linear(Hrhs, Hreg, KT, w_f1[l], DFF + fg * 128, lambda ps, pr, st=st: st.__setitem__("u", (ps, pr)))
                    for hf in range(2):
                        cs_ = slice(hf * 512, (hf + 1) * 512)
                        rs_ = Region()
                        A(lambda e: e.activation(out=sg[:, hf, :], in_=st["g"][0][:, cs_], func=AF.Silu), r=[st["g"][1][hf]], w=[rs_])
                        V(lambda e: e.tensor_tensor(out=ACTB[:, f, cs_], in0=st["u"][0][:, cs_], in1=sg[:, hf, :], op=ALU.mult), r=[st["u"][1][hf], rs_], w=[rACT[f]])
                ACrhs = lambda kt, hf: ACTB[:, kt, hf * 512:(hf + 1) * 512]
                ACreg = lambda kt, hf: rACT[kt]
                for m in range(KT):
                    def ev(ps, pr, m=m):
                        for hf in range(2):
                            cs_ = slice(hf * 512, (hf + 1) * 512)
                            V(lambda e: e.tensor_tensor(out=X[:, m, cs_], in0=ps[:, cs_], in1=X[:, m, cs_], op=ALU.add), r=[pr[hf], rX[m][hf]], w=[rX[m][hf]])
                    linear(ACrhs, ACreg, 22, w_f2[l], m * 128, ev, r0=fh * 22 * 128, chunks=[(0, 11), (11, 22)])
                P.barrier()

        stages = ["A0", "B0", "A1", "B1"][:n_stage]
        for s in stages:
            l = int(s[1])
            if s[0] == "A":
                ssm_small(l)
                V(lambda e: e.tensor_scalar(SM[:, I_ESK, 1:2], PPt[:, l, PC_KG:PC_KG + 1], float(math.sqrt(HD)), None, op0=ALU.mult), r=[smr], w=[smr])
                stage_A(l)
            else:
                stage_B(l)
        P.barrier()
        fin = list(fin_regs)
        if not need_xs:
            for kt in range(KT):
                r_ = Region()
                P.dma(P.sp, outT[kt * 128:(kt + 1) * 128, :], X[:, kt, :], reads=rX[kt], writes=[r_])
                fin.append(r_)
        P._waits(P.sp, fin, ())
        P.barrier([P.sp])
    return nc


def _consts():
    ones = np.ones((128, 128), np.float32)
    bd = np.zeros((128, 128), np.float32)
    bd[:64, :64] = 1
    bd[64:, 64:] = 1
    perm = np.zeros((128, 128), np.float32)
    for m in range(128):
        perm[(m + 64) % 128, m] = 1
    op0 = np.zeros((128, 128), np.float32)
    op0[:, :64] = 1
    op1 = np.zeros((128, 128), np.float32)
    op1[:, 64:] = 1
    cst = np.concatenate([ones, bd, perm, op0, op1], axis=1)
    iota = np.broadcast_to(np.arange(1024, dtype=np.float32)[None, :], (128, 1024)).copy()
    s = np.arange(128)[:, None]
    col = np.arange(128)[None, :]
    dist_d = (col - s).astype(np.float32)
    dist_p = (128 + col - s).astype(np.float32)
    mask_d = np.where(col >= s, 0.0, NEG).astype(np.float32)
    mask_p = np.where(col < s, 0.0, NEG).astype(np.float32)
    dm = np.concatenate([dist_d, dist_p, mask_d, mask_p], axis=1)
    m0 = (np.arange(128) < 64).astype(np.float32)
    one = np.ones(128, np.float32)
    m4 = np.stack([m0, 1 - m0, -m0, -(1 - m0), one * (D * EPS), one * (HD * EPS), 0 * one, 0 * one], axis=1).astype(np.float32)
    return cst, iota, dm, m4


def _pack_params(inp):
    L = DEPTH
    pp = np.zeros((128, L, NPC), np.float32)
    p = np.arange(128)
    for l in range(L):
        pp[:, l, PC_GMIX:PC_GMIX + 16] = inp["norm_mix_g"][l].reshape(16, 128).T
        pp[:, l, PC_GFFN:PC_GFFN + 16] = inp["norm_ffn_g"][l].reshape(16, 128).T
        pp[:, l, PC_GB:PC_GB + 32] = inp["gate_bias"][l].reshape(32, 128).T
        pp[:, l, PC_QG] = inp["q_norm_g"][l][p % 64]
        pp[:, l, PC_KG] = inp["k_norm_g"][l][p % 64]
        for hp in range(8):
            pp[:, l, PC_SINK + hp] = inp["attn_sinks"][l][2 * hp + p // 64]
        g_of = lambda gp: 2 * gp + p // 64
        for gp in range(32):
            pp[:, l, PC_LR + gp] = inp["ssm_lambda_re"][l][g_of(gp), p % 64]
            pp[:, l, PC_LI + gp] = inp["ssm_lambda_im"][l][g_of(gp), p % 64]
            pp[:, l, PC_LDT + gp] = inp["ssm_log_dt"][l][g_of(gp)]
        pp[:, l, PC_D:PC_D + 8] = inp["ssm_d"][l].reshape(8, 128).T
        pp[:, l, PC_GLUB:PC_GLUB + 8] = inp["ssm_glu_b"][l].reshape(8, 128).T
    csm = np.zeros((128, L, 2, 32, 16), np.float32)
    for l in range(L):
        for c, key in enumerate(("ssm_c_re", "ssm_c_im")):
            cc = inp[key][l].reshape(32, 2, 16, 64)
            csm[:, l, c] = cc.transpose(1, 3, 0, 2).reshape(128, 32, 16)
    bbd = np.zeros((L, 2, 128, 32, 128), np.float32)
    for l in range(L):
        for c, key in enumerate(("ssm_b_re", "ssm_b_im")):
            b = inp[key][l]
            for g in range(64):
                gp, g2, g8 = g // 2, g % 2, g % 8
                bbd[l, c, g8 * 16:(g8 + 1) * 16, gp, g2 * 64:(g2 + 1) * 64] = b[g].T
    return (pp.reshape(128, L * NPC), csm.reshape(128, L * 1024), bbd.reshape(L * 2, 128, 4096))


_CACHE = {}


def _get_nc(n_stage):
    if n_stage not in _CACHE:
        _CACHE[n_stage] = build(n_stage)
    return _CACHE[n_stage]


def _route(xs_all):
    out = []
    for c in range(NCORES):
        xr = np.zeros((128, XR_W), np.float32)
        tb = c % 4
        for k in (1, 2, 3):
            if tb - k >= 0:
                xr[:, (k - 1) * 64:k * 64] = xs_all[c - k][:, 0:64]
        if tb >= 1:
            xr[:, 192:704] = xs_all[c - 1][:, 64:576]
        out.append(xr)
    return out


def _core_inputs(inp, n_stage, xr, consts, packed):
    cst, iota, dm, m4 = consts
    pp, csm, bbd = packed
    nl_a, nl_b = (n_stage + 1) // 2, n_stage // 2
    x = inp["x"]
    maps = []
    for c in range(NCORES):
        b, tb = c // 4, c % 4
        d = {"xT": np.ascontiguousarray(x[b, tb * T:(tb + 1) * T, :].T), "w_in": inp["w_in"][:nl_a],
             "pp": pp, "csm": csm, "bbd": bbd, "cst": cst, "iota": iota, "distmask": dm, "m4": m4,
             "haloflag": np.full((128, 1), 0.0 if tb > 0 else NEG, np.float32)}
        if nl_b > 0:
            for k in ("ssm_glu_w", "w_attn_branch", "w_ssm_branch", "w_out", "w_ffn_in", "w_ffn_out"):
                d[k] = inp[k][:nl_b]
        for l in range(nl_b):
            d[f"xr{l}"] = xr[l][c]
        maps.append(d)
    return maps


def kernel(**inputs):
    inp = {k: np.ascontiguousarray(np.asarray(v, dtype=np.float32)) for k, v in inputs.items()}
    consts = _consts()
    packed = _pack_params(inp)
    ids = list(range(NCORES))
    xr = {}
    r1 = run_bass_kernel_spmd(_get_nc(1), _core_inputs(inp, 1, xr, consts, packed), core_ids=ids)
    xr[0] = _route([r1.results[c]["xs"] for c in ids])
    r2 = run_bass_kernel_spmd(_get_nc(3), _core_inputs(inp, 3, xr, consts, packed), core_ids=ids)
    xr[1] = _route([r2.results[c]["xs"] for c in ids])
    r3 = run_bass_kernel_spmd(_get_nc(4), _core_inputs(inp, 4, xr, consts, packed), core_ids=ids)
    out = np.zeros((2, 4096, D), np.float32)
    for c in ids:
        b, tb = c // 4, c % 4
        out[b, tb * T:(tb + 1) * T, :] = r3.results[c]["outT"].T
    return out
```

```python
import math
from contextlib import ExitStack

import numpy as np
import concourse.bass as bass
import concourse.mybir as mybir
from concourse.bass_utils import run_bass_kernel_spmd

F32 = mybir.dt.float32
BF16 = mybir.dt.bfloat16
U8 = mybir.dt.uint8
ALU = mybir.AluOpType
AF = mybir.ActivationFunctionType

NCORES = 8
D = 2048
KT = 16
T = 1024
DEPTH = 2
HD = 64
NQ = 16
NKV = 4
DFF = 5632
FT = DFF // 128
OFF_Q, OFF_K, OFF_V, OFF_U, OFF_G = 0, 1024, 1280, 1536, 2560
INW = 6656
EPS = 1e-6
TWO_PI = 2.0 * math.pi
NEG = -30000.0
PC_GMIX, PC_GFFN, PC_GB, PC_QG, PC_KG, PC_SINK, PC_LR, PC_LI, PC_LDT, PC_D, PC_GLUB = (
    0, 16, 32, 64, 65, 66, 74, 106, 138, 170, 178)
NPC = 186
XS_W = 576
XR_W = 704


class Region:
    __slots__ = ("w", "r")

    def __init__(self):
        self.w = {}
        self.r = {}


class Eng:
    def __init__(self, name, eng, sem, is_pe=False):
        self.name, self.eng, self.sem, self.is_pe = name, eng, sem, is_pe
        self.count = 0
        self.known = {}


class Prog:
    def __init__(self, nc, ctx, n_dma_sems=24):
        self.nc = nc
        mk = lambda n: ctx.enter_context(nc.semaphore(n))
        self.pe = Eng("pe", nc.tensor, mk("s_pe"), True)
        self.act = Eng("act", nc.scalar, mk("s_act"))
        self.dve = Eng("dve", nc.vector, mk("s_dve"))
        self.pool = Eng("pool", nc.gpsimd, mk("s_pool"))
        self.sp = Eng("sp", nc.sync, mk("s_sp"))
        self.engs = [self.pe, self.act, self.dve, self.pool, self.sp]
        self.dsems = [[mk(f"s_dma{i}"), 0] for i in range(n_dma_sems)]
        self.dnext = 0
        self.semobj = {}
        for e in self.engs:
            self.semobj[e.sem.num] = e.sem
        for s, _ in self.dsems:
            self.semobj[s.num] = s

    def _wait(self, E, s, v):
        if E.known.get(s, 0) < v:
            E.eng.wait_ge(self.semobj[s], v)
            E.known[s] = v

    def _waits(self, E, reads, writes):
        need = {}
        for r in reads:
            for s, v in r.w.items():
                need[s] = max(need.get(s, 0), v)
        for r in writes:
            for s, v in r.w.items():
                need[s] = max(need.get(s, 0), v)
            for s, v in r.r.items():
                need[s] = max(need.get(s, 0), v)
        for s, v in need.items():
            if E.is_pe and s == E.sem.num:
                continue
            self._wait(E, s, v)

    def op(self, E, fn, reads=(), writes=(), inc=True):
        self._waits(E, reads, writes)
        ins = fn(E.eng)
        if inc:
            E.count += 1
            ins.then_inc(E.sem, 1)
        c = E.count + (0 if inc else 1)
        for r in reads:
            r.r[E.sem.num] = c
        for r in writes:
            r.w = {E.sem.num: c}
            r.r = {}
        return ins

    def dma(self, Q, out, in_, reads=(), writes=(), **kw):
        self._waits(Q, reads, writes)
        slot = self.dsems[self.dnext]
        self.dnext = (self.dnext + 1) % len(self.dsems)
        sem, tot = slot
        if tot > 0:
            self._wait(Q, sem.num, tot)
        ins = Q.eng.dma_start(out=out, in_=in_, **kw)
        ins.then_inc(sem, 16)
        slot[1] = tot + 16
        for r in reads:
            r.r[sem.num] = slot[1]
        for r in writes:
            r.w = {sem.num: slot[1]}
            r.r = {}
        return ins

    def barrier(self, engs=None):
        engs = engs or self.engs
        for E in engs:
            for F in self.engs:
                if F is not E and F.count > 0:
                    self._wait(E, F.sem.num, F.count)
            for s, tot in self.dsems:
                if tot > 0:
                    self._wait(E, s.num, tot)


class _LayerView:
    def __init__(self, ap, l0):
        self.ap, self.l0 = ap, l0

    def __getitem__(self, l):
        return self.ap[l - self.l0]


def _layers(n_stage, first):
    l0 = first // 2
    return l0, (n_stage + 1) // 2 - l0, n_stage // 2 - l0


def build(n_stage, first=0, dbg=None):
    dbg = dbg or {}
    nc = bass.Bass("TRN2", target_bir_lowering=False)
    din = lambda n, s: nc.dram_tensor(n, s, F32, kind="ExternalInput").ap()
    xT = din("xT", [D, T])
    L0, nl_a, nl_b = _layers(n_stage, first)
    w_in = _LayerView(din("w_in", [nl_a, D, INW]), L0)
    if nl_b > 0:
        glu_w = _LayerView(din("ssm_glu_w", [nl_b, 1024, 1024]), L0)
        w_ab = _LayerView(din("w_attn_branch", [nl_b, 1024, D]), L0)
        w_sb = _LayerView(din("w_ssm_branch", [nl_b, 1024, D]), L0)
        w_out = _LayerView(din("w_out", [nl_b, D, D]), L0)
        w_f1 = _LayerView(din("w_ffn_in", [nl_b, D, 2 * DFF]), L0)
        w_f2 = _LayerView(din("w_ffn_out", [nl_b, DFF, D]), L0)
    pp_d = din("pp", [128, DEPTH * NPC])
    csm_d = din("csm", [128, DEPTH * 2 * 512])
    bbd_d = din("bbd", [DEPTH * 2, 128, 4096])
    cst_d = din("cst", [128, 128 * 5])
    iota_d = din("iota", [128, 1024])
    dm_d = din("distmask", [128, 512])
    m4_d = din("m4", [128, 8])
    hf_d = din("haloflag", [128, 1])
    xr_d = {l: din(f"xr{l}", [128, XR_W]) for l in range(L0, L0 + nl_b)}
    outs = {}

    def dout(n, s):
        outs[n] = nc.dram_tensor(n, s, F32, kind="ExternalOutput").ap()
        return outs[n]

    need_xs = (n_stage % 2 == 1)
    emit_x = (n_stage % 2 == 0) or (n_stage == 3)
    if need_xs:
        xs_out = dout("xs", [128, XS_W])
    if emit_x:
        outT = dout("outT", [D, T])
    rsp = nc.dram_tensor("rspill", [32, 2, 128, T], F32, kind="Internal").ap()
    for k, shp in dbg.items():
        dout("dbg_" + k, shp)

    with ExitStack() as ctx:
        P = Prog(nc, ctx)
        sbt = lambda n, s, d: ctx.enter_context(nc.sbuf_tensor(n, s, d))
        X = sbt("X", [128, KT, T], F32)
        H = sbt("H", [128, KT, T], BF16)
        U = sbt("U", [128, 8, T], BF16)
        ARENA = sbt("ARENA", [128, 67584], U8)
        WS = sbt("WS", [128, 3, KT, 128], BF16)
        CST = sbt("CST", [128, 5, 128], BF16)
        IOTA = sbt("IOTA", [128, T], F32)
        DM = sbt("DM", [128, 512], F32)
        M4 = sbt("M4", [128, 8], F32)
        HF = sbt("HF", [128, 1], F32)
        PPt = sbt("PP", [128, DEPTH, NPC], F32)
        SM = sbt("SMALL", [128, 32, 32], F32)
        SEED = sbt("SEED", [128, 22, 32], F32)
        rY = Region()
        fin_regs = []
        PSUM = [ctx.enter_context(nc.psum_tensor(f"ps{i}", [128, 1024], F32)) for i in range(4)]
        PSR = [[Region(), Region()] for _ in range(4)]
        psi = [0]

        def ps_next():
            i = psi[0]
            psi[0] = (i + 1) % 4
            return PSUM[i], PSR[i]

        def aview(off, shape, dt):
            n = 1
            for s in shape[1:]:
                n *= s
            nb = n * (4 if dt == F32 else 2)
            v = ARENA[:, off:off + nb]
            v = v.bitcast(dt) if dt != U8 else v
            if len(shape) == 3:
                v = v.rearrange("p (a b) -> p a b", b=shape[2])
            elif len(shape) == 4:
                v = v.rearrange("p (a b c) -> p a b c", b=shape[2], c=shape[3])
            elif len(shape) == 5:
                v = v.rearrange("p (a b c d) -> p a b c d", b=shape[2], c=shape[3], d=shape[4])
            return v

        XS = aview(65280, [128, XS_W], F32)
        WSF = WS[:].rearrange("p a b c -> p (a b c)").bitcast(F32)
        UF = U[:].rearrange("p a b -> p (a b)").bitcast(F32)
        A = lambda fn, r=(), w=(): P.op(P.act, fn, r, w)
        V = lambda fn, r=(), w=(): P.op(P.dve, fn, r, w)
        G = lambda fn, r=(), w=(): P.op(P.pool, fn, r, w)

        def dbg_dump(name, ap, reg):
            if name in dbg:
                P.dma(P.pool, outs["dbg_" + name], ap, reads=reg, writes=[Region()])

        rC = Region()
        P.dma(P.pool, CST[:].rearrange("p a b -> p (a b)"), cst_d[:, :], writes=[rC])
        P.dma(P.sp, IOTA[:], iota_d[:, :], writes=[rC])
        P.dma(P.sp, DM[:], dm_d[:, :], writes=[rC])
        P.dma(P.sp, M4[:], m4_d[:, :], writes=[rC])
        P.dma(P.sp, HF[:], hf_d[:, :], writes=[rC])
        P.dma(P.sp, PPt[:].rearrange("p a b -> p (a b)"), pp_d[:, :], writes=[rC])
        rX = [[Region(), Region()] for _ in range(KT)]
        for kt in range(KT):
            P.dma(P.sp, X[:, kt, :], xT[kt * 128:(kt + 1) * 128, :], writes=rX[kt])
        P.barrier()
        ONES, ONESBD, PERM = CST[:, 0, :], CST[:, 1, :], CST[:, 2, :]
        OPAD = [CST[:, 3, :], CST[:, 4, :]]
        rH = [[Region(), Region()] for _ in range(KT)]
        rU = [[Region(), Region()] for _ in range(8)]
        ws_r = [Region() for _ in range(3)]
        wsi = [0]

        def load_w(Wap, r0, nkt, c0, ncols=128):
            i = wsi[0]
            wsi[0] = (i + 1) % 3
            src = Wap[r0:r0 + nkt * 128, c0:c0 + ncols].rearrange("(kt p) m -> p kt m", p=128)
            P.dma(P.pool, WS[:, i, 0:nkt, 0:ncols], src, writes=[ws_r[i]])
            return WS[:, i], ws_r[i]

        def linear(rhs, rhs_r, nkt, Wap, c0, evac, r0=0, chunks=None):
            chunks = chunks or [(0, nkt)]
            slots = [(load_w(Wap, r0 + a * 128, b - a, c0), a, b) for a, b in chunks]
            ps, pr = ps_next()
            for hf in range(2):
                for (sl, sr), a, b in slots:
                    for kt in range(a, b):
                        P.op(P.pe, lambda e, sl=sl, kt=kt, a=a, hf=hf: e.matmul(
                            ps[:, hf * 512:(hf + 1) * 512], lhsT=sl[:, kt - a, :], rhs=rhs(kt, hf),
                            start=(kt == 0), stop=(kt == nkt - 1)),
                            reads=[sr, rhs_r(kt, hf)], writes=[pr[hf]], inc=(kt == nkt - 1))
            evac(ps, pr)


        def act_rsqrt(out, in_ps, bias_ap, r, w):
            A(lambda e: e.activation(out=out, in_=in_ps, func=AF.Ln, bias=bias_ap), r=r, w=w)
            A(lambda e: e.activation(out=out, in_=out, func=AF.Exp, scale=-0.5), r=w, w=w)

        def rmsnorm(gcol, l):
            sq = aview(0, [128, 2, 512], BF16)
            gs = aview(4096, [128, 16], F32)
            rst = aview(8192, [128, 2, 512], F32)
            rsq = [Region(), Region()]
            rgs = Region()
            rrst = [Region(), Region()]
            V(lambda e: e.tensor_scalar(gs, PPt[:, l, gcol:gcol + 16], math.sqrt(float(D)), None, op0=ALU.mult), w=[rgs])
            for hf in range(2):
                ps, pr = ps_next()
                for kt in range(KT):
                    b = kt % 2
                    A(lambda e, kt=kt, b=b, hf=hf: e.activation(out=sq[:, b, :], in_=X[:, kt, hf * 512:(hf + 1) * 512], func=AF.Square),
                      r=[rX[kt][hf]], w=[rsq[b]])
                    P.op(P.pe, lambda e, kt=kt, b=b: e.matmul(ps[:, 0:512], lhsT=ONES, rhs=sq[:, b, :], start=(kt == 0), stop=(kt == KT - 1)),
                         reads=[rsq[b]], writes=[pr[0]], inc=True)
                act_rsqrt(rst[:, hf, :], ps[:, 0:512], M4[:, 4:5], [pr[0]], [rrst[hf]])
                for kt in range(KT):
                    V(lambda e, kt=kt, hf=hf: e.scalar_tensor_tensor(out=H[:, kt, hf * 512:(hf + 1) * 512], in0=X[:, kt, hf * 512:(hf + 1) * 512],
                                                                      scalar=gs[:, kt:kt + 1], in1=rst[:, hf, :], op0=ALU.mult, op1=ALU.mult),
                      r=[rX[kt][hf], rgs, rrst[hf]], w=[rH[kt][hf]])

        Hrhs = lambda kt, hf: H[:, kt, hf * 512:(hf + 1) * 512]
        Hreg = lambda kt, hf: rH[kt][hf]

        (I_DT, I_LR, I_TH, I_RHO, I_AR, I_AI, I_DEN, I_FR, I_FI, I_T0, I_T1, I_T2, I_T3,
         I_CV, I_SV, I_RER, I_REI, I_SER, I_SEI, I_A1R, I_A1I, I_SIR, I_SII, I_R0R, I_R0I, I_ESK) = range(26)
        smr = Region()

        def sm(i):
            return SM[:, i, :]

        def sdc(k):
            return SEED[:, 2 * k, :]

        def sds(k):
            return SEED[:, 2 * k + 1, :]

        def tt(o, a, b, op):
            V(lambda e: e.tensor_tensor(out=o, in0=a, in1=b, op=op), r=[smr], w=[smr])

        def ts(o, a, s1, s2, op0, op1=None):
            if op1 is None:
                V(lambda e: e.tensor_scalar(o, a, s1, None, op0=op0), r=[smr], w=[smr])
            else:
                V(lambda e: e.tensor_scalar(o, a, s1, s2, op0=op0, op1=op1), r=[smr], w=[smr])

        def cmul(o_r, o_i, a_r, a_i, b_r, b_i):
            tt(sm(I_T2), a_r, b_r, ALU.mult)
            tt(sm(I_T3), a_i, b_i, ALU.mult)
            tt(o_r, sm(I_T2), sm(I_T3), ALU.subtract)
            tt(sm(I_T2), a_r, b_i, ALU.mult)
            tt(sm(I_T3), a_i, b_r, ALU.mult)
            tt(o_i, sm(I_T2), sm(I_T3), ALU.add)

        def csquare(oc, os_, c, s):
            tt(sm(I_T0), c, c, ALU.mult)
            tt(sm(I_T1), s, s, ALU.mult)
            V(lambda e: e.scalar_tensor_tensor(out=sm(I_T2), in0=c, scalar=2.0, in1=s, op0=ALU.mult, op1=ALU.mult), r=[smr], w=[smr])
            tt(oc, sm(I_T0), sm(I_T1), ALU.subtract)
            V(lambda e: e.tensor_copy(out=os_, in_=sm(I_T2)), r=[smr], w=[smr])
            tt(sm(I_T0), oc, oc, ALU.mult)
            tt(sm(I_T1), os_, os_, ALU.mult)
            tt(sm(I_T0), sm(I_T0), sm(I_T1), ALU.add)
            ts(sm(I_T0), sm(I_T0), -0.5, 1.5, ALU.mult, ALU.add)
            tt(oc, oc, sm(I_T0), ALU.mult)
            tt(os_, os_, sm(I_T0), ALU.mult)

        def ssm_small(l):
            ppl = lambda c: PPt[:, l, c:c + 32]
            A(lambda e: e.activation(out=sm(I_DT), in_=ppl(PC_LDT), func=AF.Exp), r=[smr], w=[smr])
            tt(sm(I_LR), ppl(PC_LR), sm(I_DT), ALU.mult)
            tt(sm(I_TH), ppl(PC_LI), sm(I_DT), ALU.mult)
            x = sm(I_LR)
            ts(sm(I_T0), x, 1.0 / 6.0, 1.0, ALU.mult, ALU.add)
            for dv in (5.0, 4.0, 3.0, 2.0):
                tt(sm(I_T0), sm(I_T0), x, ALU.mult)
                ts(sm(I_T0), sm(I_T0), 1.0 / dv, 1.0, ALU.mult, ALU.add)
            tt(sm(I_T0), sm(I_T0), x, ALU.mult)
            ts(sm(I_RHO), sm(I_T0), 1.0, None, ALU.add)
            A(lambda e: e.activation(out=sm(I_T3), in_=sm(I_TH), func=AF.Sin, scale=1.0 / 32.0), r=[smr], w=[smr])
            A(lambda e: e.activation(out=sds(0), in_=sm(I_TH), func=AF.Sin, scale=1.0 / 16.0), r=[smr], w=[smr])
            tt(sm(I_T3), sm(I_T3), sm(I_T3), ALU.mult)
            ts(sdc(0), sm(I_T3), -2.0, 1.0, ALU.mult, ALU.add)
            for _ in range(4):
                csquare(sdc(1), sds(1), sdc(0), sds(0))
                V(lambda e: e.tensor_copy(out=sdc(0), in_=sdc(1)), r=[smr], w=[smr])
                V(lambda e: e.tensor_copy(out=sds(0), in_=sds(1)), r=[smr], w=[smr])
            for k in range(1, 11):
                csquare(sdc(k), sds(k), sdc(k - 1), sds(k - 1))
            tt(sm(I_AR), sm(I_RHO), sdc(0), ALU.mult)
            tt(sm(I_AI), sm(I_RHO), sds(0), ALU.mult)

        def ssm_f(l):
            ppl = lambda c: PPt[:, l, c:c + 32]
            lr, li = ppl(PC_LR), ppl(PC_LI)
            tt(sm(I_T0), lr, lr, ALU.mult)
            tt(sm(I_T1), li, li, ALU.mult)
            tt(sm(I_DEN), sm(I_T0), sm(I_T1), ALU.add)
            V(lambda e: e.reciprocal(out=sm(I_DEN), in_=sm(I_DEN)), r=[smr], w=[smr])
            ts(sm(I_T0), sm(I_AR), -1.0, None, ALU.add)
            tt(sm(I_T1), sm(I_T0), lr, ALU.mult)
            tt(sm(I_T2), sm(I_AI), li, ALU.mult)
            tt(sm(I_T1), sm(I_T1), sm(I_T2), ALU.add)
            tt(sm(I_FR), sm(I_T1), sm(I_DEN), ALU.mult)
            tt(sm(I_T1), sm(I_AI), lr, ALU.mult)
            tt(sm(I_T2), sm(I_T0), li, ALU.mult)
            tt(sm(I_T1), sm(I_T1), sm(I_T2), ALU.subtract)
            tt(sm(I_FI), sm(I_T1), sm(I_DEN), ALU.mult)

        def base_tables(BTc, BTs, rBT):
            t1 = WSF[:, 0:512].rearrange("p (a b) -> p a b", b=16)
            t2 = WSF[:, 512:1024].rearrange("p (a b) -> p a b", b=16)
            V(lambda e: e.memset(BTc[:, :, 0:1], 1.0), w=[rBT])
            V(lambda e: e.memset(BTs[:, :, 0:1], 0.0), w=[rBT])
            for k in range(5):
                n = 1 << k
                cb = sdc(k).unsqueeze(2).to_broadcast([128, 32, n])
                sb_ = sds(k).unsqueeze(2).to_broadcast([128, 32, n])
                src_c, src_s = BTc[:, :, 0:n], BTs[:, :, 0:n]
                V(lambda e: e.tensor_tensor(out=t1[:, :, 0:n], in0=src_s, in1=sb_, op=ALU.mult), r=[rBT, smr], w=[rBT])
                V(lambda e: e.tensor_tensor(out=t2[:, :, 0:n], in0=src_c, in1=cb, op=ALU.mult), r=[rBT, smr], w=[rBT])
                V(lambda e: e.tensor_tensor(out=BTc[:, :, n:2 * n], in0=t2[:, :, 0:n], in1=t1[:, :, 0:n], op=ALU.subtract), r=[rBT], w=[rBT])
                V(lambda e: e.tensor_tensor(out=t1[:, :, 0:n], in0=src_c, in1=sb_, op=ALU.mult), r=[rBT, smr], w=[rBT])
                V(lambda e: e.tensor_tensor(out=t2[:, :, 0:n], in0=src_s, in1=cb, op=ALU.mult), r=[rBT, smr], w=[rBT])
                V(lambda e: e.tensor_tensor(out=BTs[:, :, n:2 * n], in0=t1[:, :, 0:n], in1=t2[:, :, 0:n], op=ALU.add), r=[rBT], w=[rBT])

        def gen_table(gp, cosT, sinT, rT, BTc, BTs, rBT):
            tA, tB = WSF[:, 1024:1536], WSF[:, 1536:2048]
            A(lambda e: e.activation(out=cosT[:, 0:32], in_=BTc[:, gp, :], func=AF.Copy), r=[rBT], w=[rT])
            A(lambda e: e.activation(out=sinT[:, 0:32], in_=BTs[:, gp, :], func=AF.Copy), r=[rBT], w=[rT])
            for k in range(5, 10):
                n = 1 << k
                c_ap, s_ap = SEED[:, 2 * k, gp:gp + 1], SEED[:, 2 * k + 1, gp:gp + 1]
                A(lambda e: e.activation(out=tA[:, 0:n], in_=sinT[:, 0:n], func=AF.Identity, scale=s_ap), r=[rT, smr], w=[rTS[0]])
                A(lambda e: e.activation(out=tB[:, 0:n], in_=cosT[:, 0:n], func=AF.Identity, scale=s_ap), r=[rT, smr], w=[rTS[1]])
                V(lambda e: e.scalar_tensor_tensor(out=cosT[:, n:2 * n], in0=cosT[:, 0:n], scalar=c_ap, in1=tA[:, 0:n], op0=ALU.mult, op1=ALU.subtract),
                  r=[rT, rTS[0], smr], w=[rT])
                V(lambda e: e.scalar_tensor_tensor(out=sinT[:, n:2 * n], in0=sinT[:, 0:n], scalar=c_ap, in1=tB[:, 0:n], op0=ALU.mult, op1=ALU.add),
                  r=[rT, rTS[1], smr], w=[rT])

        rTS = [Region(), Region()]

        def stage_A(l):
            P.barrier()
            rmsnorm(PC_GMIX, l)
            if l == 0:
                dbg_dump("h", H[:, 0, :], [rH[0][0], rH[0][1]])
            for m in range(8):
                def ev(ps, pr, m=m):
                    for hf in range(2):
                        A(lambda e, hf=hf: e.activation(out=U[:, m, hf * 512:(hf + 1) * 512], in_=ps[:, hf * 512:(hf + 1) * 512], func=AF.Copy),
                          r=[pr[hf]], w=[rU[m][hf]])
                linear(Hrhs, Hreg, KT, w_in[l], OFF_U + m * 128, ev)
            P.barrier()
            rXS = Region()
            ksq = aview(0, [128, 128], BF16)
            krin = aview(1024, [128, 128], F32)
            rk = Region()
            for m in range(2):
                sl, sr = load_w(w_in[l], 0, KT, OFF_K + m * 128)
                ps, pr = ps_next()
                for kt in range(KT):
                    P.op(P.pe, lambda e, kt=kt, sl=sl: e.matmul(ps[:, 0:128], lhsT=sl[:, kt, :], rhs=H[:, kt, T - 128:T], start=(kt == 0), stop=(kt == KT - 1)),
                         reads=[sr, rH[kt][1]], writes=[pr[0]], inc=(kt == KT - 1))
                A(lambda e: e.activation(out=ksq, in_=ps[:, 0:128], func=AF.Square), r=[pr[0]], w=[rk])
                P.op(P.pe, lambda e: e.matmul(ps[:, 512:640], lhsT=ONESBD, rhs=ksq, start=True, stop=True), reads=[rk], writes=[pr[1]])
                act_rsqrt(krin, ps[:, 512:640], M4[:, 5:6], [pr[1], rk], [rk])
                V(lambda e, m=m: e.scalar_tensor_tensor(out=XS[:, 64 + m * 128:64 + (m + 1) * 128], in0=ps[:, 0:128], scalar=SM[:, I_ESK, 1:2],
                                                         in1=krin, op0=ALU.mult, op1=ALU.mult), r=[pr[0], rk, smr], w=[rXS])
            ps, pr = ps_next()
            for c in range(2):
                sl, sr = load_w(w_in[l], 0, KT, OFF_V + c * 128)
                for kt in range(KT):
                    P.op(P.pe, lambda e, kt=kt, sl=sl, c=c: e.matmul(ps[:, c * 128:(c + 1) * 128], lhsT=H[:, kt, T - 128:T], rhs=sl[:, kt, :], start=(kt == 0), stop=(kt == KT - 1)),
                         reads=[sr, rH[kt][1]], writes=[pr[0]], inc=(kt == KT - 1))
            A(lambda e: e.activation(out=XS[:, 320:576], in_=ps[:, 0:256], func=AF.Copy), r=[pr[0]], w=[rXS])
            P.barrier()
            BBD = aview(0, [128, 2, 32, 128], BF16)
            BTc = aview(16384, [128, 32, 32], F32)
            BTs = aview(20480, [128, 32, 32], F32)
            TBL = [[aview(24576 + i * 8192, [128, T], F32), aview(24576 + i * 8192 + 4096, [128, T], F32)] for i in range(2)]
            rTBL = [Region(), Region()]
            rB = Region()
            rBT = Region()
            for c in range(2):
                P.dma(P.pool, BBD[:, c].rearrange("p a b -> p (a b)"), bbd_d[l * 2 + c], writes=[rB])
            base_tables(BTc, BTs, rBT)
            TB = 40960
            tmp = [aview(TB + i * 2048, [128, 512], F32) for i in range(10)]
            treg = [Region() for _ in range(10)]
            prev_end = {}
            for gp in range(32):
                kt = gp // 4
                cosT, sinT = TBL[gp % 2]
                rT = rTBL[gp % 2]
                gen_table(gp, cosT, sinT, rT, BTc, BTs, rBT)
                for hf in range(2):
                    o = ((gp * 2 + hf) % 2) * 5
                    bA, bB, bC, rre, rim = [tmp[o + i] for i in range(5)]
                    rA_, rB_, rC_, rrre, rrim = [treg[o + i] for i in range(5)]
                    cs_ = slice(hf * 512, (hf + 1) * 512)
                    cosv, sinv = cosT[:, cs_], sinT[:, cs_]
                    ps, pr = ps_next()
                    for c in range(2):
                        P.op(P.pe, lambda e, c=c: e.matmul(ps[:, c * 512:(c + 1) * 512], lhsT=BBD[:, c, gp, :], rhs=U[:, kt, cs_], start=True, stop=True),
                             reads=[rB, rU[kt][hf]], writes=[pr[c]])
                    V(lambda e: e.tensor_tensor(out=bA, in0=ps[:, 0:512], in1=cosv, op=ALU.mult), r=[pr[0], rT], w=[rA_])
                    V(lambda e: e.tensor_tensor(out=bB, in0=ps[:, 512:1024], in1=sinv, op=ALU.mult), r=[pr[1], rT], w=[rB_])
                    V(lambda e: e.tensor_tensor(out=bA, in0=bA, in1=bB, op=ALU.add), r=[rA_, rB_], w=[rA_])
                    V(lambda e: e.tensor_tensor(out=bC, in0=ps[:, 512:1024], in1=cosv, op=ALU.mult), r=[pr[1], rT], w=[rC_])
                    V(lambda e: e.tensor_tensor(out=bB, in0=ps[:, 0:512], in1=sinv, op=ALU.mult), r=[pr[0], rT, rA_], w=[rB_])
                    V(lambda e: e.tensor_tensor(out=bC, in0=bC, in1=bB, op=ALU.subtract), r=[rC_, rB_], w=[rC_])
                    rho_b = SM[:, I_RHO, gp:gp + 1].to_broadcast([128, 512])
                    if hf == 0:
                        i_re, i_im, ir = 0.0, 0.0, []
                    else:
                        (pre, pim, prr, pri) = prev_end[gp]
                        i_re, i_im, ir = pre[:, 511:512], pim[:, 511:512], [prr, pri]
                    V(lambda e: e.tensor_tensor_scan(out=rre, data0=rho_b, data1=bA, initial=i_re, op0=ALU.mult, op1=ALU.add), r=[rA_, smr] + ir, w=[rrre])
                    V(lambda e: e.tensor_tensor_scan(out=rim, data0=rho_b, data1=bC, initial=i_im, op0=ALU.mult, op1=ALU.add), r=[rC_, smr] + ir, w=[rrim])
                    prev_end[gp] = (rre, rim, rrre, rrim)
                    P.dma(P.sp, rsp[gp, 0, :, cs_], rre, reads=[rrre], writes=[Region()])
                    P.dma(P.sp, rsp[gp, 1, :, cs_], rim, reads=[rrim], writes=[Region()])
                    if hf == 1:
                        A(lambda e: e.activation(out=SM[:, I_CV, gp:gp + 1], in_=cosT[:, T - 1:T], func=AF.Copy), r=[rT, smr], w=[smr])
                        A(lambda e: e.activation(out=SM[:, I_SV, gp:gp + 1], in_=sinT[:, T - 1:T], func=AF.Copy), r=[rT, smr], w=[smr])
                        A(lambda e: e.activation(out=SM[:, I_RER, gp:gp + 1], in_=rre[:, 511:512], func=AF.Copy), r=[rrre, smr], w=[smr])
                        A(lambda e: e.activation(out=SM[:, I_REI, gp:gp + 1], in_=rim[:, 511:512], func=AF.Copy), r=[rrim, smr], w=[smr])
            cmul(sm(I_SER), sm(I_SEI), sm(I_CV), sm(I_SV), sm(I_RER), sm(I_REI))
            A(lambda e: e.activation(out=XS[:, 0:32], in_=sm(I_SER), func=AF.Copy), r=[smr], w=[rXS])
            A(lambda e: e.activation(out=XS[:, 32:64], in_=sm(I_SEI), func=AF.Copy), r=[smr], w=[rXS])
            if need_xs and l == n_stage // 2:
                r_ = Region()
                P.dma(P.sp, xs_out[:, :], XS, reads=[rXS], writes=[r_])
                fin_regs.append(r_)
            P.barrier()

        def stage_B(l):
            ppl = lambda c, n: PPt[:, l, c:c + n]
            P.barrier()
            rXR = Region()
            SPV = SM[:, 26:32, :].rearrange("p a b -> p (a b)")
            P.dma(P.sp, SPV, xr_d[l][:, 0:192], writes=[rXR])
            P.barrier()
            A(lambda e: e.activation(out=sm(I_DEN), in_=sm(I_LR), func=AF.Exp, scale=1024.0), r=[smr], w=[smr])
            tt(sm(I_A1R), sm(I_DEN), sdc(10), ALU.mult)
            tt(sm(I_A1I), sm(I_DEN), sds(10), ALU.mult)
            V(lambda e: e.tensor_copy(out=sm(I_SIR), in_=SPV[:, 128:160]), r=[rXR, smr], w=[smr])
            V(lambda e: e.tensor_copy(out=sm(I_SII), in_=SPV[:, 160:192]), r=[rXR, smr], w=[smr])
            for k in (1, 0):
                cmul(sm(I_R0R), sm(I_R0I), sm(I_SIR), sm(I_SII), sm(I_A1R), sm(I_A1I))
                tt(sm(I_SIR), sm(I_R0R), SPV[:, k * 64:k * 64 + 32], ALU.add)
                tt(sm(I_SII), sm(I_R0I), SPV[:, k * 64 + 32:k * 64 + 64], ALU.add)
            cmul(sm(I_R0R), sm(I_R0I), sm(I_SIR), sm(I_SII), sdc(0), sds(0))
            ssm_f(l)
            WC = aview(0, [128, 2, 32, 128], BF16)
            SB = aview(16384, [128, 4, 2, T], BF16)
            BTc = aview(32768, [128, 32, 32], F32)
            BTs = aview(36864, [128, 32, 32], F32)
            CS_ = aview(40960, [128, 2, 32, 16], F32)
            CF = aview(45056, [128, 2, 32, 16], F32)
            CT = aview(49152, [128, 2, 32, 16], F32)
            rc = Region()
            P.dma(P.sp, CS_[:].rearrange("p a b c -> p (a b c)"), csm_d[:, l * 1024:(l + 1) * 1024], writes=[rc])
            frb = SM[:, I_FR, :].unsqueeze(2).to_broadcast([128, 32, 16])
            fib = SM[:, I_FI, :].unsqueeze(2).to_broadcast([128, 32, 16])
            V(lambda e: e.tensor_tensor(out=CT[:, 0], in0=CS_[:, 0], in1=frb, op=ALU.mult), r=[rc, smr], w=[rc])
            V(lambda e: e.tensor_tensor(out=CT[:, 1], in0=CS_[:, 1], in1=fib, op=ALU.mult), r=[rc, smr], w=[rc])
            V(lambda e: e.tensor_tensor(out=CF[:, 0], in0=CT[:, 0], in1=CT[:, 1], op=ALU.subtract), r=[rc], w=[rc])
            V(lambda e: e.tensor_tensor(out=CT[:, 0], in0=CS_[:, 0], in1=fib, op=ALU.mult), r=[rc, smr], w=[rc])
            V(lambda e: e.tensor_tensor(out=CT[:, 1], in0=CS_[:, 1], in1=frb, op=ALU.mult), r=[rc, smr], w=[rc])
            V(lambda e: e.tensor_tensor(out=CF[:, 1], in0=CT[:, 0], in1=CT[:, 1], op=ALU.add), r=[rc], w=[rc])
            G(lambda e: e.memset(WC[:].rearrange("p a b c -> p (a b c)"), 0.0), w=[rc])
            WC6 = WC.rearrange("p c (k q) (qc g h) -> p c k q qc g h", q=4, qc=4, g=2)
            CF4 = CF.rearrange("p c (k q) h -> p c k q h", q=4)
            for c in range(2):
                for q in range(4):
                    for g2 in range(2):
                        mcol = g2 + (2 if c == 1 else 0)
                        V(lambda e, c=c, q=q, g2=g2, mcol=mcol: e.tensor_scalar(WC6[:, c, :, q, q, g2, :], CF4[:, c, :, q, :], M4[:, mcol:mcol + 1], None, op0=ALU.mult),
                          r=[rc], w=[rc])
            rBT = Region()
            base_tables(BTc, BTs, rBT)
            P.barrier()
            TBL = [[aview(40960 + i * 8192, [128, T], F32), aview(40960 + i * 8192 + 4096, [128, T], F32)] for i in range(2)]
            rTBL = [Region(), Region()]
            tmp = [aview(57344 + i * 2048, [128, 512], F32) for i in range(5)]
            treg = [Region() for _ in range(5)]
            rSB = [Region() for _ in range(4)]
            for kt in range(8):
                for q in range(4):
                    gp = kt * 4 + q
                    cosT, sinT = TBL[gp % 2]
                    rT = rTBL[gp % 2]
                    gen_table(gp, cosT, sinT, rT, BTc, BTs, rBT)
                    for hf in range(2):
                        cs_ = slice(hf * 512, (hf + 1) * 512)
                        cosv, sinv = cosT[:, cs_], sinT[:, cs_]
                        rl_re, rl_im, dec, t1, t2 = tmp
                        r_re, r_im, rdec, rt1, rt2 = treg
                        P.dma(P.sp, rl_re, rsp[gp, 0, :, cs_], writes=[r_re])
                        P.dma(P.sp, rl_im, rsp[gp, 1, :, cs_], writes=[r_im])
                        A(lambda e: e.activation(out=dec, in_=IOTA[:, cs_], func=AF.Exp, scale=SM[:, I_LR, gp:gp + 1], bias=SM[:, I_LR, gp:gp + 1]),
                          r=[smr], w=[rdec])
                        V(lambda e: e.scalar_tensor_tensor(out=rl_re, in0=dec, scalar=SM[:, I_R0R, gp:gp + 1], in1=rl_re, op0=ALU.mult, op1=ALU.add), r=[rdec, smr, r_re], w=[r_re])
                        V(lambda e: e.scalar_tensor_tensor(out=rl_im, in0=dec, scalar=SM[:, I_R0I, gp:gp + 1], in1=rl_im, op0=ALU.mult, op1=ALU.add), r=[rdec, smr, r_im], w=[r_im])
                        G(lambda e: e.tensor_tensor(out=t1, in0=cosv, in1=rl_re, op=ALU.mult), r=[rT, r_re], w=[rt1])
                        G(lambda e: e.tensor_tensor(out=t2, in0=sinv, in1=rl_im, op=ALU.mult), r=[rT, r_im], w=[rt2])
                        V(lambda e: e.tensor_tensor(out=SB[:, q, 0, cs_], in0=t1, in1=t2, op=ALU.subtract), r=[rt1, rt2], w=[rSB[q]])
                        G(lambda e: e.tensor_tensor(out=t1, in0=cosv, in1=rl_im, op=ALU.mult), r=[rT, r_im], w=[rt1])
                        G(lambda e: e.tensor_tensor(out=t2, in0=sinv, in1=rl_re, op=ALU.mult), r=[rT, r_re], w=[rt2])
                        V(lambda e: e.tensor_tensor(out=SB[:, q, 1, cs_], in0=t1, in1=t2, op=ALU.add), r=[rt1, rt2], w=[rSB[q]])
                for hf in range(2):
                    cs_ = slice(hf * 512, (hf + 1) * 512)
                    ps, pr = ps_next()
                    n = 0
                    for q in range(4):
                        for c in range(2):
                            P.op(P.pe, lambda e, q=q, c=c, n=n: e.matmul(ps[:, 0:512], lhsT=WC[:, c, kt * 4 + q, :], rhs=SB[:, q, c, cs_], start=(n == 0), stop=(n == 7)),
                                 reads=[rc, rSB[q]], writes=[pr[0]], inc=(n == 7))
                            n += 1
                    yv, y2, yt = WSF[:, 0:512], WSF[:, 512:1024], WSF[:, 2048:2560]
                    ry = rY
                    V(lambda e: e.scalar_tensor_tensor(out=yv, in0=U[:, kt, cs_], scalar=ppl(PC_D + kt, 1), in1=ps[:, 0:512], op0=ALU.mult, op1=ALU.add),
                      r=[rU[kt][hf], pr[0]], w=[ry])
                    A(lambda e: e.activation(out=y2, in_=yv, func=AF.Square), r=[ry], w=[ry])
                    V(lambda e: e.tensor_scalar(y2, y2, 0.044715, 1.0, op0=ALU.mult, op1=ALU.add), r=[ry], w=[ry])
                    V(lambda e: e.tensor_tensor(out=yt, in0=y2, in1=yv, op=ALU.mult), r=[ry], w=[ry])
                    A(lambda e: e.activation(out=y2, in_=yt, func=AF.Sigmoid, scale=2.0 * math.sqrt(2.0 / math.pi)), r=[ry], w=[ry])
                    V(lambda e: e.tensor_tensor(out=U[:, kt, cs_], in0=yv, in1=y2, op=ALU.mult), r=[ry], w=[rU[kt][hf]])
            if l == 0:
                dbg_dump("yg", U[:, 0, :], [rU[0][0], rU[0][1]])
            P.barrier()

            Q = aview(0, [128, 8, T], BF16)
            KD = aview(16384, [128, 2, 2, 1152], BF16)
            VP = aview(32768, [128, 9, 4, 2, 128], BF16)
            YA = aview(51200, [128, 8, T], BF16)
            qsq = aview(51200, [128, T], BF16)
            qrin = aview(51200 + 2048, [128, T], F32)
            rQ = [Region() for _ in range(8)]
            rKD = Region()
            rVP = Region()
            rq = Region()
            XRH = aview(25600, [128, 512], F32)
            P.dma(P.sp, XRH, xr_d[l][:, 192:704], writes=[rXR])
            G(lambda e: e.memset(VP[:].rearrange("p a b c d -> p (a b c d)"), 0.0), w=[rVP])

            def qk_evac(ps, pr, dst, gcol, ncol=T):
                A(lambda e: e.activation(out=qsq[:, 0:ncol], in_=ps[:, 0:ncol], func=AF.Square), r=[pr[0], pr[1]], w=[rq])
                ps2, pr2 = ps_next()
                for hf in range((ncol + 511) // 512):
                    w_ = min(512, ncol - hf * 512)
                    P.op(P.pe, lambda e, hf=hf, w_=w_: e.matmul(ps2[:, hf * 512:hf * 512 + w_], lhsT=ONESBD, rhs=qsq[:, hf * 512:hf * 512 + w_], start=True, stop=True),
                         reads=[rq], writes=[pr2[hf]])
                act_rsqrt(qrin[:, 0:ncol], ps2[:, 0:ncol], M4[:, 5:6], [pr2[0], pr2[1], rq], [rq])
                V(lambda e: e.scalar_tensor_tensor(out=dst, in0=ps[:, 0:ncol], scalar=gcol, in1=qrin[:, 0:ncol], op0=ALU.mult, op1=ALU.mult),
                  r=[pr[0], pr[1], rq, smr], w=[rq])

            for m in range(8):
                def ev(ps, pr, m=m):
                    qk_evac(ps, pr, Q[:, m, :], ppl(PC_QG, 1))
                    rQ[m].w = dict(rq.w)
                linear(Hrhs, Hreg, KT, w_in[l], OFF_Q + m * 128, ev)
            for m in range(2):
                def ev(ps, pr, m=m):
                    qk_evac(ps, pr, KD[:, 0, m, 128:1152], SM[:, I_ESK, 1:2])
                    rKD.w = dict(rq.w)
                linear(Hrhs, Hreg, KT, w_in[l], OFF_K + m * 128, ev)
            for m in range(2):
                V(lambda e, m=m: e.tensor_copy(out=KD[:, 0, m, 0:128], in_=XRH[:, m * 128:(m + 1) * 128]), r=[rXR, rKD], w=[rKD])
            for m in range(2):
                for c0 in range(0, 1152, 384):
                    ps, pr = ps_next()
                    P.op(P.pe, lambda e, m=m, c0=c0: e.matmul(ps[:, 0:384], lhsT=PERM, rhs=KD[:, 0, m, c0:c0 + 384], start=True, stop=True), reads=[rKD], writes=[pr[0]])
                    A(lambda e, m=m, c0=c0: e.activation(out=KD[:, 1, m, c0:c0 + 384], in_=ps[:, 0:384], func=AF.Copy), r=[pr[0]], w=[rKD])
            wv = [load_w(w_in[l], 0, KT, OFF_V + c * 128) for c in range(2)]
            for blk in range(1, 9):
                ps, pr = ps_next()
                tk = blk - 1
                for c in range(2):
                    sl, sr = wv[c]
                    for kt in range(KT):
                        P.op(P.pe, lambda e, kt=kt, sl=sl, c=c, tk=tk: e.matmul(ps[:, c * 128:(c + 1) * 128], lhsT=H[:, kt, tk * 128:(tk + 1) * 128], rhs=sl[:, kt, :], start=(kt == 0), stop=(kt == KT - 1)),
                             reads=[sr, rH[kt][tk // 4]], writes=[pr[0]], inc=(kt == KT - 1))
                A(lambda e, blk=blk: e.activation(out=VP[:, blk, :, 0, 0:64], in_=ps[:, 0:256].rearrange("p (k d) -> p k d", d=64), func=AF.Copy), r=[pr[0]], w=[rVP])
                V(lambda e, blk=blk: e.tensor_copy(out=VP[:, blk, :, 1, 64:128], in_=ps[:, 0:256].rearrange("p (k d) -> p k d", d=64)), r=[pr[0]], w=[rVP])
            A(lambda e: e.activation(out=VP[:, 0, :, 0, 0:64], in_=XRH[:, 256:512].rearrange("p (k d) -> p k d", d=64), func=AF.Copy), r=[rXR], w=[rVP])
            V(lambda e: e.tensor_copy(out=VP[:, 0, :, 1, 64:128], in_=XRH[:, 256:512].rearrange("p (k d) -> p k d", d=64)), r=[rXR], w=[rVP])
            P.barrier()
            ESK = WSF[:, 1024:1032]
            rE = Region()
            A(lambda e: e.activation(out=ESK, in_=ppl(PC_SINK, 8), func=AF.Exp), w=[rE])
            BHs = [aview(27648, [128, 256], F32), aview(28672, [128, 256], F32)]
            LGs = [aview(29696, [128, 256], F32), aview(30720, [128, 256], F32)]
            PXs = [aview(31744, [128, 256], BF16), aview(32256, [128, 256], BF16)]
            rBH = [Region(), Region()]
            rLG = [Region(), Region()]
            rPX = [Region(), Region()]
            rYA = [Region() for _ in range(8)]
            acc, racc = PSUM[0], PSR[0]
            den, rden = PSUM[1], PSR[1]
            sps, rsps = PSUM[2], PSR[2]
            s4 = [Region() for _ in range(4)]
            rDT = [Region(), Region()]
            sidx = 0
            for hp in range(8):
                kv = hp // 2
                for hf in range(2):
                    V(lambda e, hf=hf: e.memset(acc[:, hf * 512:(hf + 1) * 512], 0.0), w=[racc[hf]])
                    V(lambda e, hf=hf: e.memset(den[:, hf * 512:(hf + 1) * 512], 0.0), w=[rden[hf]])
                for e_ in range(2):
                    h = 2 * hp + e_
                    b0 = 64 * e_
                    slope = 2.0 ** (-8.0 * (h + 1) / NQ)
                    bi = h % 2
                    V(lambda e, bi=bi, slope=slope: e.scalar_tensor_tensor(out=BHs[bi], in0=DM[:, 0:256], scalar=-slope, in1=DM[:, 256:512], op0=ALU.mult, op1=ALU.add),
                      w=[rBH[bi]])
                    ksel = 0 if (kv % 2) == e_ else 1
                    for j in range(9):
                        qlo, qhi = max(0, 128 * (j - 1)), min(T, 128 * (j + 1))
                        ncol = qhi - qlo
                        bc0 = 128 if j == 0 else 0
                        si = sidx % 4
                        sidx += 1
                        S = sps[:, si * 256:si * 256 + ncol]
                        P.op(P.pe, lambda e, S=S, j=j, qlo=qlo, qhi=qhi: e.matmul(S, lhsT=KD[b0:b0 + 64, ksel, kv // 2, j * 128:(j + 1) * 128], rhs=Q[b0:b0 + 64, hp, qlo:qhi], start=True, stop=True),
                             reads=[rKD, rQ[hp]], writes=[s4[si]])
                        li = sidx % 2
                        if j == 0:
                            V(lambda e, S=S, li=li: e.scalar_tensor_tensor(out=LGs[li][:, 0:ncol], in0=S, scalar=HF[:, 0:1], in1=BHs[bi][:, bc0:bc0 + ncol], op0=ALU.add, op1=ALU.add),
                              r=[s4[si], rBH[bi]], w=[rLG[li]])
                        else:
                            V(lambda e, S=S, li=li: e.tensor_tensor(out=LGs[li][:, 0:ncol], in0=S, in1=BHs[bi][:, bc0:bc0 + ncol], op=ALU.add),
                              r=[s4[si], rBH[bi]], w=[rLG[li]])
                        A(lambda e, li=li: e.activation(out=PXs[li][:, 0:ncol], in_=LGs[li][:, 0:ncol], func=AF.Exp), r=[rLG[li]], w=[rPX[li]])
                        for part in range(ncol // 128):
                            n = (qlo // 128) + part
                            is_diag = (n == j - 1)
                            first = (e_ == 0 and not is_diag) or (e_ == 0 and n == 0 and j == 1 and False)
                            first = (e_ == 0 and j == n)
                            last = (e_ == 1 and j == n + 1)
                            pcols = PXs[li][:, part * 128:(part + 1) * 128]
                            P.op(P.pe, lambda e, n=n, pcols=pcols, first=first, last=last, j=j: e.matmul(acc[:, n * 128:(n + 1) * 128], lhsT=VP[:, j, kv, e_, :], rhs=pcols, start=False, stop=last, skip_group_check=True),
                                 reads=[rVP, rPX[li]], writes=[racc[n // 4]])
                            P.op(P.pe, lambda e, n=n, pcols=pcols, first=first, last=last: e.matmul(den[:, n * 128:(n + 1) * 128], lhsT=OPAD[e_], rhs=pcols, start=False, stop=last, skip_group_check=True),
                                 reads=[rPX[li]], writes=[rden[n // 4]])
                for hf in range(2):
                    dtm = WSF[:, hf * 512:(hf + 1) * 512]
                    rdt = rDT[hf]
                    A(lambda e, hf=hf: e.activation(out=dtm, in_=den[:, hf * 512:(hf + 1) * 512], func=AF.Ln, bias=ESK[:, hp:hp + 1]), r=[rden[hf], rE], w=[rdt])
                    A(lambda e, hf=hf: e.activation(out=dtm, in_=dtm, func=AF.Exp, scale=-1.0), r=[rdt], w=[rdt])
                    V(lambda e, hf=hf: e.tensor_tensor(out=YA[:, hp, hf * 512:(hf + 1) * 512], in0=acc[:, hf * 512:(hf + 1) * 512], in1=dtm, op=ALU.mult), r=[racc[hf], rdt], w=[rYA[hp]])
            if l == 0:
                dbg_dump("ya", YA[:, 0, :], [rYA[0]])
            P.barrier()
            YS = aview(32768, [128, 8, T], BF16)
            gate = aview(0, [128, T], F32)
            rYS = [Region() for _ in range(8)]
            rg = Region()
            Urhs = lambda kt, hf: U[:, kt, hf * 512:(hf + 1) * 512]
            Ureg = lambda kt, hf: rU[kt][hf]
            for m in range(8):
                def ev(ps, pr, m=m):
                    A(lambda e: e.activation(out=gate, in_=ps[:, :], func=AF.Sigmoid, bias=ppl(PC_GLUB + m, 1)), r=[pr[0], pr[1]], w=[rg])
                    V(lambda e: e.tensor_tensor(out=YS[:, m, :], in0=U[:, m, :], in1=gate, op=ALU.mult), r=[rg, rU[m][0], rU[m][1]], w=[rYS[m]])
                linear(Urhs, Ureg, 8, glu_w[l], m * 128, ev)
            P.barrier()
            MG = aview(0, [128, KT, T], BF16)
            rMG = [Region() for _ in range(KT)]
            rt = [Region(), Region(), Region()]
            YArhs = lambda kt, hf: YA[:, kt, hf * 512:(hf + 1) * 512]
            YAreg = lambda kt, hf: rYA[kt]
            YSrhs = lambda kt, hf: YS[:, kt, hf * 512:(hf + 1) * 512]
            YSreg = lambda kt, hf: rYS[kt]
            for m in range(KT):
                st = {}

                def ev_ga(ps, pr, m=m, st=st):
                    st["ga"] = (ps, pr)

                def ev_a(ps, pr, m=m, st=st):
                    st["a"] = (ps, pr)

                def ev_gs(ps, pr, m=m, st=st):
                    st["gs"] = (ps, pr)

                def ev_s(ps, pr, m=m, st=st):
                    st["s"] = (ps, pr)
                linear(Hrhs, Hreg, KT, w_in[l], OFF_G + m * 128, ev_ga)
                linear(YArhs, YAreg, 8, w_ab[l], m * 128, ev_a)
                linear(Hrhs, Hreg, KT, w_in[l], OFF_G + D + m * 128, ev_gs)
                linear(YSrhs, YSreg, 8, w_sb[l], m * 128, ev_s)
                for hf in range(2):
                    cs_ = slice(hf * 512, (hf + 1) * 512)
                    g1, t1, g2 = UF[:, 0:512], UF[:, 512:1024], UF[:, 1024:1536]
                    A(lambda e: e.activation(out=g1, in_=st["ga"][0][:, cs_], func=AF.Sigmoid, bias=ppl(PC_GB + m, 1)), r=[st["ga"][1][hf]], w=[rt[0]])
                    V(lambda e: e.tensor_tensor(out=t1, in0=st["a"][0][:, cs_], in1=g1, op=ALU.mult), r=[st["a"][1][hf], rt[0]], w=[rt[1]])
                    A(lambda e: e.activation(out=g2, in_=st["gs"][0][:, cs_], func=AF.Sigmoid, bias=ppl(PC_GB + 16 + m, 1)), r=[st["gs"][1][hf]], w=[rt[2]])
                    V(lambda e: e.tensor_tensor(out=g2, in0=st["s"][0][:, cs_], in1=g2, op=ALU.mult), r=[st["s"][1][hf], rt[2]], w=[rt[2]])
                    V(lambda e: e.tensor_tensor(out=MG[:, m, cs_], in0=t1, in1=g2, op=ALU.add), r=[rt[1], rt[2]], w=[rMG[m]])
            P.barrier()
            MGrhs = lambda kt, hf: MG[:, kt, hf * 512:(hf + 1) * 512]
            MGreg = lambda kt, hf: rMG[kt]
            for m in range(KT):
                def ev(ps, pr, m=m):
                    for hf in range(2):
                        cs_ = slice(hf * 512, (hf + 1) * 512)
                        V(lambda e: e.tensor_tensor(out=X[:, m, cs_], in0=ps[:, cs_], in1=X[:, m, cs_], op=ALU.add), r=[pr[hf], rX[m][hf]], w=[rX[m][hf]])
                linear(MGrhs, MGreg, KT, w_out[l], m * 128, ev)
            if l == 0:
                dbg_dump("xmid", X[:, 0, :], [rX[0][0], rX[0][1]])
            P.barrier()
            rmsnorm(PC_GFFN, l)
            P.barrier()
            ACTB = aview(16384, [128, 22, T], BF16)
            sg = aview(0, [128, 2, 512], F32)
            rACT = [Region() for _ in range(22)]
            for fh in range(2):
                for f in range(22):
                    fg = fh * 22 + f
                    st = {}
                    linear(Hrhs, Hreg, KT, w_f1[l], fg * 128, lambda ps, pr, st=st: st.__setitem__("g", (ps, pr)))
                    linear(Hrhs, Hreg, KT, w_f1[l], DFF + fg * 128, lambda ps, pr, st=st: st.__setitem__("u", (ps, pr)))
                    for hf in range(2):
                        cs_ = slice(hf * 512, (hf + 1) * 512)
                        rs_ = Region()
                        A(lambda e: e.activation(out=sg[:, hf, :], in_=st["g"][0][:, cs_], func=AF.Silu), r=[st["g"][1][hf]], w=[rs_])
                        V(lambda e: e.tensor_tensor(out=ACTB[:, f, cs_], in0=st["u"][0][:, cs_], in1=sg[:, hf, :], op=ALU.mult), r=[st["u"][1][hf], rs_], w=[rACT[f]])
                ACrhs = lambda kt, hf: ACTB[:, kt, hf * 512:(hf + 1) * 512]
                ACreg = lambda kt, hf: rACT[kt]
                for m in range(KT):
                    def ev(ps, pr, m=m):
                        for hf in range(2):
                            cs_ = slice(hf * 512, (hf + 1) * 512)
                            V(lambda e: e.tensor_tensor(out=X[:, m, cs_], in0=ps[:, cs_], in1=X[:, m, cs_], op=ALU.add), r=[pr[hf], rX[m][hf]], w=[rX[m][hf]])
                    linear(ACrhs, ACreg, 22, w_f2[l], m * 128, ev, r0=fh * 22 * 128, chunks=[(0, 11), (11, 22)])
                P.barrier()

        stages = ["A0", "B0", "A1", "B1"][first:n_stage]
        for s in stages:
            l = int(s[1])
            if s[0] == "A":
                ssm_small(l)
                V(lambda e: e.tensor_scalar(SM[:, I_ESK, 1:2], PPt[:, l, PC_KG:PC_KG + 1], float(math.sqrt(HD)), None, op0=ALU.mult), r=[smr], w=[smr])
                stage_A(l)
            else:
                stage_B(l)
        P.barrier()
        fin = list(fin_regs)
        if emit_x:
            for kt in range(KT):
                r_ = Region()
                P.dma(P.sp, outT[kt * 128:(kt + 1) * 128, :], X[:, kt, :], reads=rX[kt], writes=[r_])
                fin.append(r_)
        P._waits(P.sp, fin, ())
        P.barrier([P.sp])
    return nc


def _consts():
    ones = np.ones((128, 128), np.float32)
    bd = np.zeros((128, 128), np.float32)
    bd[:64, :64] = 1
    bd[64:, 64:] = 1
    perm = np.zeros((128, 128), np.float32)
    for m in range(128):
        perm[(m + 64) % 128, m] = 1
    op0 = np.zeros((128, 128), np.float32)
    op0[:, :64] = 1
    op1 = np.zeros((128, 128), np.float32)
    op1[:, 64:] = 1
    cst = np.concatenate([ones, bd, perm, op0, op1], axis=1)
    iota = np.broadcast_to(np.arange(1024, dtype=np.float32)[None, :], (128, 1024)).copy()
    s = np.arange(128)[:, None]
    col = np.arange(128)[None, :]
    dist_d = (col - s).astype(np.float32)
    dist_p = (128 + col - s).astype(np.float32)
    mask_d = np.where(col >= s, 0.0, NEG).astype(np.float32)
    mask_p = np.where(col < s, 0.0, NEG).astype(np.float32)
    dm = np.concatenate([dist_d, dist_p, mask_d, mask_p], axis=1)
    m0 = (np.arange(128) < 64).astype(np.float32)
    one = np.ones(128, np.float32)
    m4 = np.stack([m0, 1 - m0, -m0, -(1 - m0), one * (D * EPS), one * (HD * EPS), 0 * one, 0 * one], axis=1).astype(np.float32)
    return cst, iota, dm, m4


def _pack_params(inp):
    L = DEPTH
    pp = np.zeros((128, L, NPC), np.float32)
    p = np.arange(128)
    for l in range(L):
        pp[:, l, PC_GMIX:PC_GMIX + 16] = inp["norm_mix_g"][l].reshape(16, 128).T
        pp[:, l, PC_GFFN:PC_GFFN + 16] = inp["norm_ffn_g"][l].reshape(16, 128).T
        pp[:, l, PC_GB:PC_GB + 32] = inp["gate_bias"][l].reshape(32, 128).T
        pp[:, l, PC_QG] = inp["q_norm_g"][l][p % 64]
        pp[:, l, PC_KG] = inp["k_norm_g"][l][p % 64]
        for hp in range(8):
            pp[:, l, PC_SINK + hp] = inp["attn_sinks"][l][2 * hp + p // 64]
        g_of = lambda gp: 2 * gp + p // 64
        for gp in range(32):
            pp[:, l, PC_LR + gp] = inp["ssm_lambda_re"][l][g_of(gp), p % 64]
            pp[:, l, PC_LI + gp] = inp["ssm_lambda_im"][l][g_of(gp), p % 64]
            pp[:, l, PC_LDT + gp] = inp["ssm_log_dt"][l][g_of(gp)]
        pp[:, l, PC_D:PC_D + 8] = inp["ssm_d"][l].reshape(8, 128).T
        pp[:, l, PC_GLUB:PC_GLUB + 8] = inp["ssm_glu_b"][l].reshape(8, 128).T
    csm = np.zeros((128, L, 2, 32, 16), np.float32)
    for l in range(L):
        for c, key in enumerate(("ssm_c_re", "ssm_c_im")):
            cc = inp[key][l].reshape(32, 2, 16, 64)
            csm[:, l, c] = cc.transpose(1, 3, 0, 2).reshape(128, 32, 16)
    bbd = np.zeros((L, 2, 128, 32, 128), np.float32)
    for l in range(L):
        for c, key in enumerate(("ssm_b_re", "ssm_b_im")):
            b = inp[key][l]
            for g in range(64):
                gp, g2, g8 = g // 2, g % 2, g % 8
                bbd[l, c, g8 * 16:(g8 + 1) * 16, gp, g2 * 64:(g2 + 1) * 64] = b[g].T
    return (pp.reshape(128, L * NPC), csm.reshape(128, L * 1024), bbd.reshape(L * 2, 128, 4096))


_CACHE = {}


def _get_nc(n_stage, first=0):
    if (n_stage, first) not in _CACHE:
        _CACHE[(n_stage, first)] = build(n_stage, first)
    return _CACHE[(n_stage, first)]


def _route(xs_all):
    out = []
    for c in range(NCORES):
        xr = np.zeros((128, XR_W), np.float32)
        tb = c % 4
        for k in (1, 2, 3):
            if tb - k >= 0:
                xr[:, (k - 1) * 64:k * 64] = xs_all[c - k][:, 0:64]
        if tb >= 1:
            xr[:, 192:704] = xs_all[c - 1][:, 64:576]
        out.append(xr)
    return out


def _core_inputs(inp, n_stage, xr, consts, packed, first=0, xin=None):
    cst, iota, dm, m4 = consts
    pp, csm, bbd = packed
    L0, nl_a, nl_b = _layers(n_stage, first)
    x = inp["x"]
    maps = []
    for c in range(NCORES):
        b, tb = c // 4, c % 4
        xT = xin[c] if xin is not None else np.ascontiguousarray(x[b, tb * T:(tb + 1) * T, :].T)
        d = {"xT": xT, "w_in": inp["w_in"][L0:L0 + nl_a],
             "pp": pp, "csm": csm, "bbd": bbd, "cst": cst, "iota": iota, "distmask": dm, "m4": m4,
             "haloflag": np.full((128, 1), 0.0 if tb > 0 else NEG, np.float32)}
        if nl_b > 0:
            for k in ("ssm_glu_w", "w_attn_branch", "w_ssm_branch", "w_out", "w_ffn_in", "w_ffn_out"):
                d[k] = inp[k][L0:L0 + nl_b]
        for l in range(L0, L0 + nl_b):
            d[f"xr{l}"] = xr[l][c]
        maps.append(d)
    return maps


def kernel(**inputs):
    inp = {k: np.ascontiguousarray(np.asarray(v, dtype=np.float32)) for k, v in inputs.items()}
    consts = _consts()
    packed = _pack_params(inp)
    ids = list(range(NCORES))
    xr = {}
    r1 = run_bass_kernel_spmd(_get_nc(1), _core_inputs(inp, 1, xr, consts, packed), core_ids=ids)
    xr[0] = _route([r1.results[c]["xs"] for c in ids])
    r2 = run_bass_kernel_spmd(_get_nc(3), _core_inputs(inp, 3, xr, consts, packed), core_ids=ids)
    xr[1] = _route([r2.results[c]["xs"] for c in ids])
    x1 = [np.ascontiguousarray(r2.results[c]["outT"]) for c in ids]
    r3 = run_bass_kernel_spmd(_get_nc(4, 2), _core_inputs(inp, 4, xr, consts, packed, first=2, xin=x1), core_ids=ids)
    out = np.zeros((2, 4096, D), np.float32)
    for c in ids:
        b, tb = c // 4, c % 4
        out[b, tb * T:(tb + 1) * T, :] = r3.results[c]["outT"].T
    return out
```
